# Optimizing a Trainium2 kernel written in Bass

```python
import math
import jax, jax.numpy as jnp
from jax import lax
import numpy as np

D_MODEL = 1024
BATCH = 2
SEQ = 8192
DEPTH = 1

RET_HEADS = 8
RET_QK_DIM = 128
RET_V_DIM = 256
RET_QK_WIDTH = RET_HEADS * RET_QK_DIM
RET_V_WIDTH = RET_HEADS * RET_V_DIM
CHUNK = 128
ROPE_BASE = 10000.0
CONV_DIM = D_MODEL
CONV_WIDTH = 31
CONV_HALF = CONV_WIDTH // 2
D_FF = 4 * D_MODEL
N_BRANCHES = 2
EPS = 1e-6
GN_EPS = 1e-5

SPLITS = np.cumsum([RET_QK_WIDTH, RET_QK_WIDTH, RET_V_WIDTH, RET_V_WIDTH, 2 * CONV_DIM])[:].tolist()
IN_WIDTH = 2 * RET_QK_WIDTH + 2 * RET_V_WIDTH + 2 * CONV_DIM + N_BRANCHES * D_MODEL

kernel_name = "hybrid_retention_conformer_block"


def rmsnorm(x, w):
    xf = x.astype(jnp.float32)
    y = xf * lax.rsqrt(jnp.mean(xf * xf, axis=-1, keepdims=True) + EPS)
    return (y * w.astype(jnp.float32)).astype(x.dtype)


def layernorm(x, w, b):
    xf = x.astype(jnp.float32)
    mu = jnp.mean(xf, axis=-1, keepdims=True)
    var = jnp.mean(jnp.square(xf - mu), axis=-1, keepdims=True)
    y = (xf - mu) * lax.rsqrt(var + GN_EPS)
    return (y * w.astype(jnp.float32) + b.astype(jnp.float32)).astype(x.dtype)


def rotary(t, cos, sin):
    t1, t2 = jnp.split(t, 2, axis=-1)
    return jnp.concatenate([t1 * cos - t2 * sin, t2 * cos + t1 * sin], axis=-1)


def retention_one_direction(q, k, v, log_gamma, strict):
    dt = q.dtype
    t = jnp.arange(CHUNK)
    diff = (t[:, None] - t[None, :])
    mask = diff > 0 if strict else diff >= 0
    lg = log_gamma.astype(jnp.float32)
    decay = jnp.where(mask[None], jnp.exp(lg[:, None, None] * jnp.maximum(diff, 0)[None].astype(jnp.float32)), 0.0)
    tf = t.astype(jnp.float32)
    xi = jnp.exp(lg[:, None] * (tf + 1.0)).astype(dt)
    zeta = jnp.exp(lg[:, None] * (CHUNK - 1.0 - tf)).astype(dt)
    g_chunk = jnp.exp(lg * CHUNK).astype(dt)
    scores = jnp.einsum('bhnid,bhnjd->bhnij', q, k) * decay.astype(dt)[None, :, None]
    inner = jnp.einsum('bhnij,bhnjv->bhniv', scores, v)
    upd = jnp.einsum('bhncd,bhnce->bhnde', k * zeta[None, :, None, :, None], v)
    upd = jnp.moveaxis(upd, 2, 0)

    def step(state, u):
        return g_chunk[None, :, None, None] * state + u, state

    _, prev_states = lax.scan(step, jnp.zeros_like(upd[0]), upd)
    prev_states = jnp.moveaxis(prev_states, 0, 2)
    cross = jnp.einsum('bhncd,bhnde->bhnce', q * xi[None, :, None, :, None], prev_states)
    return inner + cross


def bidirectional_retention(q, k, v, log_gamma_fwd, log_gamma_bwd):
    b, s, h, _ = q.shape
    n = s // CHUNK

    def to_chunks(t):
        return t.reshape(b, n, CHUNK, h, -1).transpose(0, 3, 1, 2, 4)

    def from_chunks(t):
        return t.transpose(0, 2, 3, 1, 4).reshape(b, s, h, -1)

    def flip(t):
        return t[:, ::-1]

    fwd = from_chunks(retention_one_direction(to_chunks(q), to_chunks(k), to_chunks(v), log_gamma_fwd, False))
    bwd = flip(from_chunks(retention_one_direction(to_chunks(flip(q)), to_chunks(flip(k)), to_chunks(flip(v)),
                                                   log_gamma_bwd, True)))
    return fwd + bwd


def head_groupnorm(o, w):
    of = o.astype(jnp.float32)
    mu = jnp.mean(of, axis=-1, keepdims=True)
    var = jnp.mean(jnp.square(of - mu), axis=-1, keepdims=True)
    y = ((of - mu) * lax.rsqrt(var + GN_EPS)).reshape(o.shape[0], o.shape[1], -1)
    return (y * w.astype(jnp.float32)).astype(o.dtype)


def setup_inputs(seed: int = 0) -> dict:
    key = jax.random.key(seed)
    ks = jax.random.split(key, 20)
    f32 = jnp.float32

    def nrm(k, shape, scale):
        return jax.random.normal(k, shape, f32) * scale

    def gain(k, n):
        return 1.0 + 0.02 * jax.random.normal(k, (DEPTH, n), f32)

    base = jnp.log(-jnp.log1p(-jnp.exp2(-5.0 - jnp.arange(RET_HEADS, dtype=f32))))
    ret_decay_raw = base[None, None, :] + 0.05 * jax.random.normal(ks[3], (DEPTH, 2, RET_HEADS), f32)
    return {
        "x": jax.random.normal(ks[0], (BATCH, SEQ, D_MODEL), f32),
        "norm1_w": gain(ks[1], D_MODEL),
        "w_in": nrm(ks[2], (DEPTH, D_MODEL, IN_WIDTH), D_MODEL ** -0.5),
        "ret_decay_raw": ret_decay_raw,
        "ret_gn_w": gain(ks[4], RET_V_WIDTH),
        "w_ret_o": nrm(ks[5], (DEPTH, RET_V_WIDTH, D_MODEL), RET_V_WIDTH ** -0.5),
        "b_glu": nrm(ks[6], (DEPTH, 2 * CONV_DIM), 0.02),
        "conv_w": nrm(ks[7], (DEPTH, CONV_WIDTH, CONV_DIM), CONV_WIDTH ** -0.5),
        "conv_b": nrm(ks[8], (DEPTH, CONV_DIM), 0.02),
        "conv_ln_w": gain(ks[9], CONV_DIM),
        "conv_ln_b": nrm(ks[10], (DEPTH, CONV_DIM), 0.02),
        "w_conv_o": nrm(ks[11], (DEPTH, CONV_DIM, D_MODEL), CONV_DIM ** -0.5),
        "b_conv_o": nrm(ks[12], (DEPTH, D_MODEL), 0.02),
        "w_out": nrm(ks[13], (DEPTH, D_MODEL, D_MODEL), D_MODEL ** -0.5),
        "norm2_w": gain(ks[14], D_MODEL),
        "w_mlp1": nrm(ks[15], (DEPTH, D_MODEL, D_FF), D_MODEL ** -0.5),
        "w_mlp2": nrm(ks[16], (DEPTH, D_FF, D_MODEL), D_FF ** -0.5),
        "norm_f_w": 1.0 + 0.02 * jax.random.normal(ks[17], (D_MODEL,), f32),
    }


def reference(x, norm1_w, w_in, ret_decay_raw, ret_gn_w, w_ret_o, b_glu, conv_w, conv_b, conv_ln_w,
              conv_ln_b, w_conv_o, b_conv_o, w_out, norm2_w, w_mlp1, w_mlp2, norm_f_w):
    b, s, _ = x.shape
    inv_freq = ROPE_BASE ** (-jnp.arange(0, RET_QK_DIM, 2, dtype=jnp.float32) / RET_QK_DIM)
    ang = jnp.arange(s, dtype=jnp.float32)[:, None] * inv_freq[None, :]
    cos = jnp.cos(ang)[:, None, :].astype(x.dtype)
    sin = jnp.sin(ang)[:, None, :].astype(x.dtype)
    k_scale = RET_QK_DIM ** -0.5

    for l in range(DEPTH):
        h = rmsnorm(x, norm1_w[l])
        proj = h @ w_in[l]
        q, k, v, g, glu, gates = jnp.split(proj, SPLITS, axis=-1)
        q = rotary(q.reshape(b, s, RET_HEADS, RET_QK_DIM), cos, sin)
        k = rotary(k.reshape(b, s, RET_HEADS, RET_QK_DIM), cos, sin) * k_scale
        v = v.reshape(b, s, RET_HEADS, RET_V_DIM)
        log_gamma = -jnp.exp(ret_decay_raw[l].astype(jnp.float32))
        o = bidirectional_retention(q, k, v, log_gamma[0], log_gamma[1])
        o = head_groupnorm(o, ret_gn_w[l]) * jax.nn.silu(g)
        y_ret = o @ w_ret_o[l]
        glu = glu + b_glu[l]
        ga, gb = jnp.split(glu, 2, axis=-1)
        u = ga * jax.nn.sigmoid(gb)
        u = lax.conv_general_dilated(u, conv_w[l][:, None, :].astype(u.dtype), window_strides=(1,),
                                     padding=[(CONV_HALF, CONV_HALF)],
                                     dimension_numbers=('NWC', 'WIO', 'NWC'),
                                     feature_group_count=CONV_DIM) + conv_b[l]
        u = jax.nn.silu(layernorm(u, conv_ln_w[l], conv_ln_b[l]))
        y_conv = u @ w_conv_o[l] + b_conv_o[l]
        gate_ret, gate_conv = jnp.split(jax.nn.sigmoid(gates), N_BRANCHES, axis=-1)
        x = x + (gate_ret * y_ret + gate_conv * y_conv) @ w_out[l]
        h = rmsnorm(x, norm2_w[l])
        x = x + jnp.square(jax.nn.relu(h @ w_mlp1[l])) @ w_mlp2[l]
    return rmsnorm(x, norm_f_w)
```

```python
import math
from contextlib import ExitStack

import numpy as np
import concourse.bass as bass
import concourse.mybir as mybir
from concourse.bass_utils import run_bass_kernel_spmd

F32 = mybir.dt.float32
BF16 = mybir.dt.bfloat16
AF = mybir.ActivationFunctionType
ALU = mybir.AluOpType
AX = mybir.AxisListType

NCORES = 8
D = 1024
NTOK = 2048
NCH = 16
NH = 8
KS = 128 ** -0.5
LNKS = math.log(KS)
EPS = 1e-6
GN_EPS = 1e-5


class Reg:
    __slots__ = ("name", "w", "r", "dw", "dr")

    def __init__(self, name):
        self.name = name
        self.w = {}
        self.r = {}
        self.dw = {}
        self.dr = {}


class V:
    __slots__ = ("ap", "reg")

    def __init__(self, ap, reg):
        self.ap = ap
        self.reg = reg

    def __getitem__(self, idx):
        return V(self.ap[idx], self.reg)


class T:
    def __init__(self, h, name):
        self.h = h
        self.reg = Reg(name)

    def __getitem__(self, idx):
        return V(self.h[idx], self.reg)

    def v(self, ap):
        return V(ap, self.reg)

    def carve(self, idx, name):
        return V(self.h[idx], Reg(name))


class Prog:
    ENGS = ("pe", "act", "dve", "pool", "sp")

    def __init__(self, nc):
        self.nc = nc
        self.ops = {e: [] for e in self.ENGS}
        self.dma_cnt = {}
        self.n_t = 0

    def sb(self, shape, dt, name, stack=None):
        self.n_t += 1
        name = f"{name}_{self.n_t}"
        if stack is None:
            h = self.nc.alloc_sbuf_tensor(name, list(shape), dt)
        else:
            h = stack.enter_context(self.nc.sbuf_tensor(name, list(shape), dt))
        return T(h, name)

    def ps(self, shape, dt, name, stack):
        self.n_t += 1
        name = f"{name}_{self.n_t}"
        h = stack.enter_context(self.nc.psum_tensor(name, list(shape), dt))
        return T(h, name)

    def dram(self, shape, dt, name, kind="Internal"):
        return T(self.nc.dram_tensor(name, list(shape), dt, kind=kind), name)

    def op(self, eng, emit, reads=(), writes=(), dma_key=None, inc=16):
        lst = self.ops[eng]
        idx = len(lst)
        is_dma = dma_key is not None
        de, dd = {}, {}
        for r in reads:
            r = r.reg
            for e, i in r.w.items():
                if e != eng or eng != "pe" or is_dma:
                    if de.get(e, -1) < i:
                        de[e] = i
            for k, c in r.dw.items():
                if dd.get(k, -1) < c:
                    dd[k] = c
        for w in writes:
            w = w.reg
            for e, i in list(w.r.items()) + list(w.w.items()):
                if e != eng or is_dma or eng != "pe":
                    if de.get(e, -1) < i:
                        de[e] = i
            for k, c in list(w.dr.items()) + list(w.dw.items()):
                if dd.get(k, -1) < c:
                    dd[k] = c
        cnt = None
        if is_dma:
            cnt = self.dma_cnt.get(dma_key, 0) + inc
            self.dma_cnt[dma_key] = cnt
        for r in reads:
            r = r.reg
            if is_dma:
                r.dr[dma_key] = cnt
            else:
                r.r[eng] = idx
        for w in writes:
            w = w.reg
            w.r.clear()
            w.dr.clear()
            w.w.clear()
            w.dw.clear()
            if is_dma:
                w.dw[dma_key] = cnt
            else:
                w.w[eng] = idx
        lst.append(dict(emit=emit, de=de, dd=dd, dma_key=dma_key, inc=inc))
        return idx

    def barrier(self):
        last = {}
        for e in self.ENGS:
            nd = [i for i, o in enumerate(self.ops[e]) if o["dma_key"] is None and o["emit"] is not None]
            if nd:
                last[e] = nd[-1]
        for e in self.ENGS:
            de = {e2: i for e2, i in last.items() if e2 != e}
            self.ops[e].append(dict(emit=None, de=de, dd=dict(self.dma_cnt), dma_key=None, inc=0))

    def emit(self):
        nc = self.nc
        sig = {e: set() for e in self.ENGS}
        for e in self.ENGS:
            for o in self.ops[e]:
                for e2, i2 in o["de"].items():
                    sig[e2].add(i2)
        last = {}
        for e in self.ENGS:
            if e != "sp":
                nd = [i for i, o in enumerate(self.ops[e]) if o["dma_key"] is None and o["emit"] is not None]
                if nd:
                    sig[e].add(nd[-1])
                    last[e] = nd[-1]
        cnts = {}
        for e in self.ENGS:
            c = 0
            m = {}
            for i in range(len(self.ops[e])):
                if i in sig[e]:
                    c += 1
                    m[i] = c
            cnts[e] = m
        esem = {e: nc.alloc_semaphore(f"sem_{e}") for e in self.ENGS}
        dsem = {k: nc.alloc_semaphore(f"dsem_{k}") for k in self.dma_cnt}
        self.stats = {e: len(self.ops[e]) for e in self.ENGS}
        self.stats["nsem"] = len(esem) + len(dsem)
        prog = self

        def replay(e, eng):
            seen_e = {}
            seen_d = {}
            nwait = 0
            for i, o in enumerate(prog.ops[e]):
                for e2, i2 in o["de"].items():
                    v = cnts[e2][i2]
                    if seen_e.get(e2, 0) < v:
                        eng.wait_ge(esem[e2], v)
                        seen_e[e2] = v
                        nwait += 1
                for k, c in o["dd"].items():
                    if seen_d.get(k, 0) < c:
                        eng.wait_ge(dsem[k], c)
                        seen_d[k] = c
                        nwait += 1
                if o["emit"] is None:
                    continue
                ins = o["emit"](eng)
                if o["dma_key"] is not None:
                    ins.then_inc(dsem[o["dma_key"]], o["inc"])
                    assert i not in sig[e]
                elif i in sig[e]:
                    ins.then_inc(esem[e], 1)
            if e == "sp":
                for e2, i2 in last.items():
                    eng.wait_ge(esem[e2], cnts[e2][i2])
                for k, c in prog.dma_cnt.items():
                    eng.wait_ge(dsem[k], c)
            prog.stats["w_" + e] = nwait

        with nc.Block() as block:
            @block.tensor
            def _(eng):
                replay("pe", eng)

            @block.scalar
            def _(eng):
                replay("act", eng)

            @block.vector
            def _(eng):
                replay("dve", eng)

            @block.gpsimd
            def _(eng):
                replay("pool", eng)

            @block.sync
            def _(eng):
                replay("sp", eng)

    def dma(self, q, out, in_, key, **kw):
        return self.op(q, lambda e: e.dma_start(out=out.ap, in_=in_.ap, **kw),
                       reads=[in_], writes=[out], dma_key=key)

    def mm(self, out, lhsT, rhs, start=True, stop=True):
        return self.op("pe", lambda e: e.matmul(out.ap, lhsT.ap, rhs.ap, start=start, stop=stop),
                       reads=[lhsT, rhs], writes=[out])

    def tr(self, out, in_, ident):
        return self.op("pe", lambda e: e.transpose(out.ap, in_.ap, ident.ap),
                       reads=[in_, ident], writes=[out])

    def act(self, out, in_, func, bias=None, scale=None, accum=None):
        kw = {}
        rd = [in_]
        wr = [out]
        if bias is not None:
            if isinstance(bias, V):
                kw["bias"] = bias.ap
                rd.append(bias)
            else:
                kw["bias"] = bias
        if scale is not None:
            if isinstance(scale, V):
                kw["scale"] = scale.ap
                rd.append(scale)
            else:
                kw["scale"] = scale
        if accum is not None:
            kw["accum_out"] = accum.ap
            wr.append(accum)
        return self.op("act", lambda e: e.activation(out.ap, in_.ap, func, **kw), reads=rd, writes=wr)

    def tt(self, eng, out, a, b, op):
        return self.op(eng, lambda e: e.tensor_tensor(out.ap, a.ap, b.ap, op), reads=[a, b], writes=[out])

    def ts(self, eng, out, a, s1, op0, s2=None, op1=None):
        rd = [a]
        s1a = s1.ap if isinstance(s1, V) else s1
        s2a = s2.ap if isinstance(s2, V) else s2
        if isinstance(s1, V):
            rd.append(s1)
        if isinstance(s2, V):
            rd.append(s2)
        kw = {}
        if op1 is not None:
            kw["op1"] = op1
        return self.op(eng, lambda e: e.tensor_scalar(out.ap, a.ap, s1a, s2a, op0, **kw), reads=rd, writes=[out])

    def stt(self, eng, out, a, s, b, op0, op1):
        rd = [a, b]
        sa = s.ap if isinstance(s, V) else s
        if isinstance(s, V):
            rd.append(s)
        return self.op(eng, lambda e: e.scalar_tensor_tensor(out.ap, a.ap, sa, b.ap, op0, op1),
                       reads=rd, writes=[out])

    def red(self, eng, out, in_, op=None):
        op = op or ALU.add
        return self.op(eng, lambda e: e.tensor_reduce(out.ap, in_.ap, AX.X, op), reads=[in_], writes=[out])

    def cp(self, eng, out, in_):
        if eng == "act":
            return self.op(eng, lambda e: e.copy(out.ap, in_.ap), reads=[in_], writes=[out])
        return self.op(eng, lambda e: e.tensor_copy(out.ap, in_.ap), reads=[in_], writes=[out])

    def memset(self, eng, out, val):
        return self.op(eng, lambda e: e.memset(out.ap, val), writes=[out])


def bc(v, axis, shape):
    return V(v.ap.unsqueeze(axis).broadcast_to(list(shape)), v.reg)


C_P1, C_P2, C_C1, C_C2 = 0, 128, 256, 384
C_P, C_127P, C_TF, C_TB = 512, 513, 514, 530
CST_N = 546
PV_N1W, PV_BGLU, PV_CONVB, PV_LNW, PV_LNB, PV_BCO, PV_N2W = 0, 8, 24, 32, 40, 48, 56
PV_ROWS = 64


def build(stop_after=99, dbg=False, nheads=NH, sub=9):
    nc = bass.Bass("TRN2", target_bir_lowering=False)
    P = Prog(nc)
    ein = "ExternalInput"
    x_d = P.dram([NTOK, D], F32, "x", ein)
    xh_d = P.dram([32, D], F32, "xh", ein)
    hmask_d = P.dram([128, 32], F32, "hmask", ein)
    rope_d = P.dram([128, 2 * NCH * 64], F32, "rope", ein)
    cst_d = P.dram([128, CST_N], F32, "cst", ein)
    cce_d = P.dram([128, 8], F32, "cce", ein)
    ccm_d = P.dram([128, 8], F32, "ccm", ein)
    n1w_d = P.dram([1, D], F32, "norm1_w", ein)
    win_d = P.dram([1, D, 10240], F32, "w_in", ein)
    raw_d = P.dram([1, 2, NH], F32, "ret_decay_raw", ein)
    gnw_d = P.dram([1, 2048], F32, "ret_gn_w", ein)
    wro_d = P.dram([1, 2048, D], F32, "w_ret_o", ein)
    bglu_d = P.dram([1, 2048], F32, "b_glu", ein)
    cw_d = P.dram([1, 31, D], F32, "conv_w", ein)
    cb_d = P.dram([1, D], F32, "conv_b", ein)
    lnw_d = P.dram([1, D], F32, "conv_ln_w", ein)
    lnb_d = P.dram([1, D], F32, "conv_ln_b", ein)
    wco_d = P.dram([1, D, D], F32, "w_conv_o", ein)
    bco_d = P.dram([1, D], F32, "b_conv_o", ein)
    wout_d = P.dram([1, D, D], F32, "w_out", ein)
    n2w_d = P.dram([1, D], F32, "norm2_w", ein)
    w1_d = P.dram([1, D, 4096], F32, "w_mlp1", ein)
    w2_d = P.dram([1, 4096, D], F32, "w_mlp2", ein)
    nfw_d = P.dram([D], F32, "norm_f_w", ein)
    out_d = P.dram([NTOK, D], F32, "out", "ExternalOutput")
    ogt_d = P.dram([2048, NTOK], BF16, "ogt_scr")
    yc_d = P.dram([D, NTOK], BF16, "yc_scr")
    ain_d = [P.dram([256, 256], F32, f"ain{i}") for i in range(2)]
    aout_d = [P.dram([4 * 256, 256], F32, f"aout{i}") for i in range(2)]
    dbg_d = {}

    def dbg_out(name, shape, dt=F32):
        dbg_d[name] = P.dram(shape, dt, "dbg_" + name, "ExternalOutput")
        return dbg_d[name]

    win = win_d.h.ap()[0]

    def wsrc(t, r0, nk, c0, ncol, ap2=None):
        a = ap2 if ap2 is not None else win
        return V(a[r0:r0 + nk * 128, c0:c0 + ncol].rearrange("(k p) n -> p k n", p=128), t.reg)

    top = ExitStack()
    with top:
        hT = P.sb([128, 8, 2080], BF16, "hT", top)
        ident = P.sb([128, 128], BF16, "ident", top)
        identf = P.sb([128, 128], F32, "identf", top)
        pv = P.sb([128, PV_ROWS], F32, "pv", top)
        cneg = P.sb([128, 1], F32, "cneg", top)

        P.memset("pool", identf[:, :], 0.0)
        P.op("pool", lambda e: e.affine_select(identf.h[:, :], identf.h[:, :], pattern=[[-1, 128]],
                                               compare_op=ALU.not_equal, fill=1.0, base=0,
                                               channel_multiplier=1),
             reads=[identf], writes=[identf])
        P.cp("dve", ident[:, :], identf[:, :])
        P.memset("pool", cneg[:, :], -0.5)

        with ExitStack() as s0:
            pvr = P.sb([PV_ROWS, 128], F32, "pvr", s0)
            for (row, dd_, n) in ((PV_N1W, n1w_d, 8), (PV_BGLU, bglu_d, 16), (PV_CONVB, cb_d, 8),
                                  (PV_LNW, lnw_d, 8), (PV_LNB, lnb_d, 8), (PV_BCO, bco_d, 8),
                                  (PV_N2W, n2w_d, 8)):
                P.dma("sp", pvr[row:row + n, :],
                      dd_.v(dd_.h.ap().rearrange("a (k p) -> (a k) p", p=128)), "par")
            ptp = P.ps([128, 512], F32, "ptp", s0)
            P.tr(ptp[:, 0:PV_ROWS], pvr[:, :], identf[0:PV_ROWS, 0:PV_ROWS])
            P.cp("dve", pv[:, :], ptp[:, 0:PV_ROWS])

            xt = [P.sb([128, 4, D], F32, f"xt{i}", s0) for i in range(2)]
            xsb = [P.sb([128, 4, D], BF16, f"xsb{i}", s0) for i in range(2)]
            junk = P.sb([128, D], F32, "junk", s0)
            ss = P.sb([128, 20], F32, "ss", s0)
            rstd = P.sb([128, 20], F32, "rstd", s0)
            ptr = [P.ps([128, 8, 128], BF16, f"ptr{i}", s0) for i in range(2)]
            P.memset("dve", ss[:, :], 0.0)
            P.memset("pool", xt[0][:, :, :], 0.0)
            for g in range(5):
                b = g % 2
                if g < 4:
                    P.dma("sp", xt[b][:, :, :],
                          x_d.v(x_d.h.ap()[g * 512:(g + 1) * 512, :].rearrange("(c p) d -> p c d", p=128)),
                          f"x{b}")
                    nsub = 4
                else:
                    P.dma("sp", xt[b][0:32, 0, :], xh_d[:, :], f"x{b}")
                    nsub = 1
                for c in range(nsub):
                    P.act(junk[:, :], xt[b][:, c, :], AF.Square, accum=ss[:, g * 4 + c:g * 4 + c + 1])
                sl = slice(g * 4, g * 4 + nsub)
                P.ts("dve", rstd[:, sl], ss[:, sl], 1.0 / D, ALU.mult, EPS, ALU.add)
                P.tt("pool", rstd[:, sl], rstd[:, sl], V(cneg.h[:, 0:1].broadcast_to([128, nsub]), cneg.reg), ALU.pow)
                for c in range(nsub):
                    P.ts("dve", xsb[b][:, c, :], xt[b][:, c, :], rstd[:, g * 4 + c:g * 4 + c + 1], ALU.mult)
                    pt = ptr[c % 2]
                    for k in range(8):
                        P.tr(pt[:, k, :], xsb[b][:, c, k * 128:(k + 1) * 128], ident[:, :])
                    n1 = bc(pv[:, PV_N1W:PV_N1W + 8], 2, [128, 8, 128])
                    if g < 4:
                        col = (g * 4 + c) * 128
                        P.tt("dve", hT[:, :, col:col + 128], pt[:, :, :], n1, ALU.mult)
                    else:
                        n1h = bc(pv[:, PV_N1W:PV_N1W + 8], 2, [128, 8, 32])
                        P.tt("dve", hT[:, :, 2048:2080], pt[:, :, 0:32], n1h, ALU.mult)
        P.barrier()
        if dbg:
            d = dbg_out("hT", [128, 8 * 2080], BF16)
            P.dma("sp", d.v(d.h.ap().rearrange("p (k t) -> p k t", k=8)), hT[:, :, :], "dbg")
        if stop_after <= 0:
            P.emit()
            return nc, P, dbg_d

        with ExitStack() as s1:
            cst = P.sb([128, CST_N], F32, "cst", s1)
            P.dma("sp", cst[:, :], cst_d[:, :], "par")
            rope = P.sb([128, 2, NCH, 64], F32, "rope", s1)
            P.dma("sp", rope[:, :, :, :], rope_d.v(rope_d.h.ap().rearrange("p (a n f) -> p a n f", a=2, n=NCH)), "par")
            gnwh = [P.sb([128, 256], F32, f"gnwh{i}", s1) for i in range(2)]
            lg = P.sb([128, 16], F32, "lg", s1)
            P.dma("sp", lg[:, :], raw_d.v(raw_d.h.ap().rearrange("a d h -> a (d h)").partition_broadcast(128)), "par")
            cce = P.sb([128, 8], F32, "cce", s1)
            ccm = P.sb([128, 8], F32, "ccm", s1)
            P.dma("sp", cce[:, :], cce_d[:, :], "par")
            P.dma("sp", ccm[:, :], ccm_d[:, :], "par")
            for t_ in (cst, rope, lg, cce, ccm):
                t_.reg.dw["par"] = P.dma_cnt["par"]

            P.act(lg[:, :], lg[:, :], AF.Exp)
            P.ts("dve", lg[:, :], lg[:, :], -1.0, ALU.mult)
            DT = P.sb([128, 128], F32, "DT", s1)
            XF = P.sb([128, 128], F32, "XF", s1)
            XB = P.sb([128, 128], F32, "XB", s1)
            zf = P.sb([128, NH], F32, "zf", s1)
            zb = P.sb([128, NH], F32, "zb", s1)
            zF = P.sb([128, NH, NCH], F32, "zF", s1)
            zB = P.sb([128, NH, NCH], F32, "zB", s1)
            g128 = P.sb([128, 16], F32, "g128", s1)
            cc = P.sb([128, 2, 4, NH], F32, "cc", s1)
            tmpc = P.sb([128, 128], F32, "tmpc", s1)
            for h in range(NH):
                lf = lg[:, h:h + 1]
                lb = lg[:, 8 + h:9 + h]
                P.ts("dve", tmpc[:, 0:16], cst[:, C_TF:C_TF + 16], lf, ALU.mult, LNKS, ALU.add)
                P.act(zF[:, h, :], tmpc[:, 0:16], AF.Exp)
                P.ts("dve", tmpc[:, 16:32], cst[:, C_TB:C_TB + 16], lb, ALU.mult, LNKS, ALU.add)
                P.act(zB[:, h, :], tmpc[:, 16:32], AF.Exp)
                P.ts("dve", tmpc[:, 32:33], cst[:, C_127P:C_127P + 1], lf, ALU.mult, LNKS, ALU.add)
                P.act(zf[:, h:h + 1], tmpc[:, 32:33], AF.Exp)
                P.ts("dve", tmpc[:, 33:34], cst[:, C_P:C_P + 1], lb, ALU.mult, LNKS, ALU.add)
                P.act(zb[:, h:h + 1], tmpc[:, 33:34], AF.Exp)
            P.act(g128[:, :], lg[:, :], AF.Exp, scale=128.0)
            for d_ in range(2):
                for r in range(4):
                    P.ts("dve", tmpc[:, 64:72], lg[:, d_ * 8:d_ * 8 + 8], cce[:, d_ * 4 + r:d_ * 4 + r + 1], ALU.mult)
                    P.act(tmpc[:, 72:80], tmpc[:, 64:72], AF.Exp)
                    P.ts("dve", cc[:, d_, r, :], tmpc[:, 72:80], ccm[:, d_ * 4 + r:d_ * 4 + r + 1], ALU.mult)

            psF = [P.ps([128, 512], F32, f"psF{i}", s1) for i in range(6)]
            psT = [P.ps([128, 1024], BF16, f"psT{i}", s1) for i in range(2)]

            wb = [P.sb([128, 8, 768], BF16, f"wb{i}", s1) for i in range(2)]
            qkT = P.sb([128, 2, NTOK], BF16, "qkT", s1)
            qfT = P.sb([128, NTOK], BF16, "qfT", s1)
            qbT = P.sb([128, NTOK], BF16, "qbT", s1)
            kfb = P.sb([128, 2, NCH, 128], BF16, "kfb", s1)
            vb = P.sb([128, NCH, 256], BF16, "vb", s1)
            sg = P.sb([128, NCH, 256], BF16, "sg", s1)
            kFB = P.sb([128, 2, NCH, 128], BF16, "kFB", s1)
            rawqk = [P.sb([128, 4, 2, 2, 64], F32, f"rawqk{i}", s1) for i in range(2)]
            th = [P.sb([128, 256], F32, f"th{i}", s1) for i in range(2)]
            ra = P.sb([128, 4, 2, 64], F32, "ra", s1)
            rb = P.sb([128, 4, 2, 64], F32, "rb", s1)
            rc = P.sb([128, 4, 2, 64], F32, "rc", s1)
            rd_ = P.sb([128, 4, 2, 64], F32, "rd", s1)
            rot = [P.sb([128, 4, 2, 2, 64], BF16, f"rot{i}", s1) for i in range(2)]
            AS = P.sb([128, 2, 256], F32, "AS", s1)
            agin = P.sb([128, 4, 2, 256], F32, "agin", s1)
            Sst = P.sb([128, 2, 256], F32, "Sst", s1)
            Rf32 = [P.sb([128, 2, 256], F32, f"Rf32_{i}", s1) for i in range(2)]
            Rb16 = [[P.sb([128, 256], BF16, f"Rb16_{d_}_{n}", s1) for n in range(NCH)] for d_ in range(2)]
            sd = [P.sb([128, 4, 128], BF16, f"sd{i}", s1) for i in range(2)]
            osb = [P.sb([128, 4, 256], F32, f"osb{i}", s1) for i in range(2)]
            st1 = P.sb([128, 8], F32, "st1", s1)
            st2 = P.sb([128, 8], F32, "st2", s1)
            mean = P.sb([128, 8], F32, "mean", s1)
            var = P.sb([128, 8], F32, "var", s1)
            og1 = P.sb([128, 4, 256], F32, "og1", s1)
            ogb = [P.sb([128, 4, 256], BF16, f"ogb{i}", s1) for i in range(2)]
            ogT = P.sb([128, 2, NTOK], BF16, "ogT", s1)

            def head_consts(h):
                lf = lg[:, h:h + 1]
                lb = lg[:, 8 + h:9 + h]
                P.ts("dve", tmpc[:, :], cst[:, C_P1:C_P1 + 128], lf, ALU.mult, LNKS, ALU.add)
                P.stt("dve", tmpc[:, :], cst[:, C_P2:C_P2 + 128], lb, tmpc[:, :], ALU.mult, ALU.add)
                P.act(DT[:, :], tmpc[:, :], AF.Exp)
                P.act(XF[:, :], cst[:, C_C1:C_C1 + 128], AF.Exp, scale=lf)
                P.act(XB[:, :], cst[:, C_C2:C_C2 + 128], AF.Exp, scale=lb)
                gw = gnwh[h % 2]
                P.dma("sp", gw[:, :], gnw_d.v(gnw_d.h.ap()[0:1, h * 256:(h + 1) * 256].partition_broadcast(128)),
                      f"gnw{h % 2}")
                P.ts("dve", gw[:, :], gw[:, :], 0.5, ALU.mult)

            cnt = {"F": 0, "T": 0, "sd": 0, "rq": 0, "ro": 0, "ob": 0, "og": 0}

            def nxt(key, lst):
                i = cnt[key] % len(lst)
                cnt[key] += 1
                return lst[i]

            def load_w(h):
                w = wb[h % 2]
                k_ = f"w{h % 2}"
                P.dma("pool", w[:, :, 0:128], wsrc(win_d, 0, 8, h * 128, 128), k_)
                P.dma("pool", w[:, :, 128:256], wsrc(win_d, 0, 8, 1024 + h * 128, 128), k_)
                P.dma("pool", w[:, :, 256:512], wsrc(win_d, 0, 8, 2048 + h * 256, 256), k_)
                P.dma("pool", w[:, :, 512:768], wsrc(win_d, 0, 8, 4096 + h * 256, 256), k_)

            def proj(h):
                w = wb[h % 2]
                for cg in range(4):
                    rq = nxt("rq", rawqk)
                    ro = nxt("ro", rot)
                    for ci in range(4):
                        c = cg * 4 + ci
                        pa = nxt("F", psF)
                        for k in range(8):
                            P.mm(pa[:, :], hT[:, k, c * 128:(c + 1) * 128], w[:, k, 0:512],
                                 start=(k == 0), stop=(k == 7))
                        pb = nxt("F", psF)
                        for k in range(8):
                            P.mm(pb[:, 0:256], hT[:, k, c * 128:(c + 1) * 128], w[:, k, 512:768],
                                 start=(k == 0), stop=(k == 7))
                        P.cp("act", V(rq.h[:, ci].rearrange("p a b f -> p (a b f)"), rq.reg), pa[:, 0:256])
                        P.cp("act", vb[:, c, :], pa[:, 256:512])
                        t_ = th[c % 2]
                        P.act(t_[:, :], pb[:, 0:256], AF.Tanh, scale=0.5)
                        P.stt("dve", sg[:, c, :], t_[:, :], 1.0, pb[:, 0:256], ALU.add, ALU.mult)
                    cos4 = bc(rope[:, 0, cg * 4:(cg + 1) * 4, :], 2, [128, 4, 2, 64])
                    sin4 = bc(rope[:, 1, cg * 4:(cg + 1) * 4, :], 2, [128, 4, 2, 64])
                    t1 = rq[:, :, :, 0, :]
                    t2 = rq[:, :, :, 1, :]
                    P.tt("dve", ra[:, :, :, :], t1, cos4, ALU.mult)
                    P.tt("dve", rb[:, :, :, :], t2, sin4, ALU.mult)
                    P.tt("dve", ro[:, :, :, 0, :], ra[:, :, :, :], rb[:, :, :, :], ALU.subtract)
                    P.tt("pool", rc[:, :, :, :], t2, cos4, ALU.mult)
                    P.tt("pool", rd_[:, :, :, :], t1, sin4, ALU.mult)
                    P.tt("pool", ro[:, :, :, 1, :], rc[:, :, :, :], rd_[:, :, :, :], ALU.add)
                    krot = V(ro.h[:, :, 1].rearrange("p c a f -> p c (a f)"), ro.reg)
                    csl = slice(cg * 4, (cg + 1) * 4)
                    P.ts("dve", kfb[:, 0, csl, :], krot, zf[:, h:h + 1], ALU.mult)
                    P.ts("dve", kfb[:, 1, csl, :], krot, zb[:, h:h + 1], ALU.mult)
                    P.tt("pool", kFB[:, 0, csl, :], krot, bc(zF[:, h, csl], 2, [128, 4, 128]), ALU.mult)
                    P.tt("pool", kFB[:, 1, csl, :], krot, bc(zB[:, h, csl], 2, [128, 4, 128]), ALU.mult)
                    pt = nxt("T", psT)
                    ptv = V(pt.h[:, :].rearrange("p (c a f) -> p c a f", c=4, a=2), pt.reg)
                    for ci in range(4):
                        for a in range(2):
                            P.tr(ptv[:, ci, a, :], V(ro.h[:, ci, a].rearrange("p a f -> p (a f)"), ro.reg),
                                 ident[:, :])
                    tsl = slice(cg * 512, (cg + 1) * 512)
                    P.cp("act", V(qkT.h[:, :, tsl].rearrange("p a (c f) -> p c a f", c=4), qkT.reg), ptv[:, :, :, :])
                    q4 = V(qkT.h[:, 0, tsl].rearrange("p (c f) -> p c f", c=4), qkT.reg)
                    P.tt("pool", V(qfT.h[:, tsl].rearrange("p (c f) -> p c f", c=4), qfT.reg), q4,
                         bc(XF[:, :], 1, [128, 4, 128]), ALU.mult)
                    P.tt("pool", V(qbT.h[:, tsl].rearrange("p (c f) -> p c f", c=4), qbT.reg), q4,
                         bc(XB[:, :], 1, [128, 4, 128]), ALU.mult)

            def phaseA(h):
                s = h % 2
                pu = nxt("F", psF)
                for d_ in range(2):
                    for c in range(NCH):
                        P.mm(pu[:, d_ * 256:(d_ + 1) * 256], kFB[:, d_, c, :], vb[:, c, :],
                             start=(c == 0), stop=(c == NCH - 1))
                P.cp("act", V(AS.h[:, :, :].rearrange("p d v -> p (d v)"), AS.reg), pu[:, :])
                P.dma("sp", ain_d[s].v(ain_d[s].h.ap().rearrange("(d p) v -> p d v", p=128)), AS[:, :, :], f"ain{s}")
                P.op("pool", lambda e: e.collective_compute(
                    "AllGather", ALU.bypass, replica_groups=[[0, 1, 2, 3], [4, 5, 6, 7]],
                    ins=[ain_d[s].h.ap().opt()], outs=[aout_d[s].h.ap().opt()]),
                    reads=[ain_d[s]], writes=[aout_d[s]], dma_key=f"cc{s}", inc=1)
                P.dma("sp", agin[:, :, :, :],
                      aout_d[s].v(aout_d[s].h.ap().rearrange("(r d p) v -> p r d v", p=128, d=2)), "agl")

            def attn(h):
                for d_ in range(2):
                    P.ts("dve", Sst[:, d_, :], agin[:, 0, d_, :], cc[:, d_, 0, h:h + 1], ALU.mult)
                    for r in range(1, 4):
                        P.stt("dve", Sst[:, d_, :], agin[:, r, d_, :], cc[:, d_, r, h:h + 1], Sst[:, d_, :],
                              ALU.mult, ALU.add)
                cur = Sst
                P.cp("pool", Rb16[0][0][:, :], Sst[:, 0, :])
                P.cp("pool", Rb16[1][NCH - 1][:, :], Sst[:, 1, :])
                for step in range(NCH - 1):
                    pu = nxt("F", psF)
                    nw = Rf32[step % 2]
                    for d_ in range(2):
                        n = step if d_ == 0 else NCH - 1 - step
                        P.mm(pu[:, d_ * 256:(d_ + 1) * 256], kfb[:, d_, n, :], vb[:, n, :])
                    for d_ in range(2):
                        n = step if d_ == 0 else NCH - 1 - step
                        nn = n + 1 if d_ == 0 else n - 1
                        P.stt("dve", nw[:, d_, :], cur[:, d_, :], g128[:, d_ * 8 + h:d_ * 8 + h + 1],
                              pu[:, d_ * 256:(d_ + 1) * 256], ALU.mult, ALU.add)
                        P.cp("pool", Rb16[d_][nn][:, :], nw[:, d_, :])
                    cur = nw
                order = sorted(range(NCH), key=lambda n: (max(n, NCH - 1 - n), n))
                if sub < 4:
                    return
                for gi in range(4):
                    grp = order[gi * 4:(gi + 1) * 4]
                    pS = nxt("F", psF)
                    for i, n in enumerate(grp):
                        tsl = slice(n * 128, (n + 1) * 128)
                        P.mm(pS[:, i * 128:(i + 1) * 128], qkT[:, 1, tsl], qkT[:, 0, tsl])
                    sd4 = nxt("sd", sd)
                    P.tt("dve", sd4[:, :, :], V(pS.h[:, :].rearrange("p (c f) -> p c f", c=4), pS.reg),
                         bc(DT[:, :], 1, [128, 4, 128]), ALU.mult)
                    ob = nxt("ob", osb)
                    for pr in range(2):
                        pO = nxt("F", psF)
                        for i2 in range(2):
                            i = pr * 2 + i2
                            n = grp[i]
                            tsl = slice(n * 128, (n + 1) * 128)
                            po = pO[:, i2 * 256:(i2 + 1) * 256]
                            P.mm(po, sd4[:, i, :], vb[:, n, :], start=True, stop=False)
                            P.mm(po, qfT[:, tsl], Rb16[0][n][:, :], start=False, stop=False)
                            P.mm(po, qbT[:, tsl], Rb16[1][n][:, :], start=False, stop=True)
                        P.cp("act", V(ob.h[:, pr * 2:pr * 2 + 2, :].rearrange("p c v -> p (c v)"), ob.reg), pO[:, :])
                    if sub >= 5:
                        gnorm(h, ob, grp)
                if sub < 8:
                    return
                P.dma("sp", ogt_d.v(ogt_d.h.ap()[h * 256:(h + 1) * 256, :].rearrange("(a p) t -> p a t", p=128)),
                      ogT[:, :, :], "ogs")

            def gnorm(h, ob, grp):
                P.red("dve", st1[:, 0:4], ob[:, :, :])
                P.tt("pool", og1[:, :, :], ob[:, :, :], ob[:, :, :], ALU.mult)
                P.red("dve", st2[:, 0:4], og1[:, :, :])
                P.ts("dve", mean[:, 0:4], st1[:, 0:4], 1.0 / 256, ALU.mult)
                P.tt("dve", var[:, 0:4], mean[:, 0:4], mean[:, 0:4], ALU.mult)
                P.stt("dve", var[:, 0:4], st2[:, 0:4], 1.0 / 256, var[:, 0:4], ALU.mult, ALU.subtract)
                P.ts("dve", var[:, 0:4], var[:, 0:4], GN_EPS, ALU.add)
                if sub < 6:
                    return
                P.tt("pool", var[:, 0:4], var[:, 0:4], V(cneg.h[:, 0:1].broadcast_to([128, 4]), cneg.reg), ALU.pow)
                P.tt("dve", og1[:, :, :], ob[:, :, :], bc(mean[:, 0:4], 2, [128, 4, 256]), ALU.subtract)
                P.tt("dve", og1[:, :, :], og1[:, :, :], bc(var[:, 0:4], 2, [128, 4, 256]), ALU.mult)
                P.tt("pool", og1[:, :, :], og1[:, :, :], bc(gnwh[h % 2][:, :], 1, [128, 4, 256]), ALU.mult)
                if sub < 7:
                    return
                o_b = nxt("og", ogb)
                for i, n in enumerate(grp):
                    P.tt("pool", o_b[:, i, :], og1[:, i, :], sg[:, n, :], ALU.mult)
                pt = nxt("T", psT)
                ptv = V(pt.h[:, :].rearrange("p (c a f) -> p c a f", c=4, a=2), pt.reg)
                for i, n in enumerate(grp):
                    for a in range(2):
                        P.tr(ptv[:, i, a, :], o_b[:, i, a * 128:(a + 1) * 128], ident[:, :])
                for i, n in enumerate(grp):
                    P.cp("act", ogT[:, :, n * 128:(n + 1) * 128], ptv[:, i, :, :])

            load_w(0)
            for h in range(nheads):
                if h + 1 < nheads:
                    load_w(h + 1)
                head_consts(h)
                if sub >= 1:
                    proj(h)
                if sub >= 2:
                    phaseA(h)
                if sub >= 3:
                    attn(h)
                if dbg and h == 0:
                    d = dbg_out("qkT", [128, 2 * NTOK], BF16)
                    P.dma("sp", d.v(d.h.ap().rearrange("p (a t) -> p a t", a=2)), qkT[:, :, :], "dbg")
                    d = dbg_out("S", [128, 512], F32)
                    P.dma("sp", d.v(d.h.ap().rearrange("p (a t) -> p a t", a=2)), Sst[:, :, :], "dbg")
                    d = dbg_out("AS", [128, 512], F32)
                    P.dma("sp", d.v(d.h.ap().rearrange("p (a t) -> p a t", a=2)), AS[:, :, :], "dbg")
        P.barrier()
        if dbg:
            d = dbg_out("ogt", [2048, NTOK], BF16)
            P.dma("sp", d[:, :], ogt_d[:, :], "dbg")
        if stop_after <= 1:
            P.emit()
            return nc, P, dbg_d

        def wsrc2(t, ap2, r0, nk, c0, ncol):
            return V(ap2[r0:r0 + nk * 128, c0:c0 + ncol].rearrange("(k p) n -> p k n", p=128), t.reg)

        wro2 = wro_d.h.ap()[0]
        wco2 = wco_d.h.ap()[0]
        wout2 = wout_d.h.ap()[0]
        w12 = w1_d.h.ap()[0]
        w22 = w2_d.h.ap()[0]

        with ExitStack() as s2:
            psF = [P.ps([128, 512], F32, f"ps2F{i}", s2) for i in range(8)]
            cnt = {"F": 0, "wg": 0, "dg": 0, "sq": 0, "t": 0, "yc": 0}

            def nxt(key, lst):
                i = cnt[key] % len(lst)
                cnt[key] += 1
                return lst[i]

            uT = P.sb([128, 8, 2080], BF16, "uT", s2)
            wg = [P.sb([128, 8, 2, 128], BF16, f"wg{i}", s2) for i in range(3)]
            wco = P.sb([128, 8, D], BF16, "wco", s2)
            cwr = P.sb([31, D], F32, "cwr", s2)
            cw = P.sb([128, 8, 31], F32, "cw", s2)
            dg = [P.sb([128, 31, 128], BF16, f"dg{i}", s2) for i in range(2)]
            cpre = P.sb([128, 8, 512], F32, "cpre", s2)
            sq = [P.sb([128, 512], F32, f"sq{i}", s2) for i in range(2)]
            onesf = P.sb([128, 128], F32, "onesf", s2)
            hmask = P.sb([128, 32], F32, "hmask", s2)
            hbb = P.sb([128, 8], F32, "hbb", s2)
            tha = [P.sb([128, 512], F32, f"tha{i}", s2) for i in range(2)]
            tga = [P.sb([128, 512], F32, f"tga{i}", s2) for i in range(2)]
            uh = P.sb([128, 32], F32, "uh", s2)
            mean_t = P.sb([128, 512], F32, "mean_t", s2)
            rstd_t = P.sb([128, 512], F32, "rstd_t", s2)
            msq = P.sb([128, 512], F32, "msq", s2)
            tn1 = [P.sb([128, 512], F32, f"tn1{i}", s2) for i in range(2)]
            tn2 = [P.sb([128, 512], F32, f"tn2{i}", s2) for i in range(2)]
            thz = [P.sb([128, 512], F32, f"thz{i}", s2) for i in range(2)]
            cT = P.sb([128, 8, 512], BF16, "cT", s2)
            ycT = [P.sb([128, 8, 512], BF16, f"ycT{i}", s2) for i in range(2)]

            P.dma("pool", wco[:, :, :], wsrc2(wco_d, wco2, 0, 8, 0, D), "wco")
            P.dma("sp", cwr[:, :], cw_d.v(cw_d.h.ap()[0]), "cwr")
            P.dma("sp", hmask[:, :], hmask_d[:, :], "hm")
            P.memset("dve", onesf[:, :], 1.0)
            P.ts("dve", hbb[:, :], pv[:, PV_BGLU + 8:PV_BGLU + 16], 0.5, ALU.mult)
            pcw = nxt("F", psF)
            for cc in range(8):
                P.tr(pcw[:, cc * 32:cc * 32 + 31], cwr[:, cc * 128:(cc + 1) * 128], identf[0:31, 0:31])
            P.ts("dve", cw[:, :, :], V(pcw.h[:, 0:256].rearrange("p (c j) -> p c j", c=8)[:, :, 0:31], pcw.reg),
                 0.5, ALU.mult)

            def load_wg(cc):
                w = wg[cc % 3]
                P.dma("pool", w[:, :, 0, :], wsrc(win_d, 0, 8, 6144 + cc * 128, 128), f"wg{cc % 3}")
                P.dma("pool", w[:, :, 1, :], wsrc(win_d, 0, 8, 7168 + cc * 128, 128), f"wg{cc % 3}")

            load_wg(0)
            load_wg(1)
            for cc in range(8):
                if cc + 2 < 8:
                    load_wg(cc + 2)
                w = wg[cc % 3]
                ba = pv[:, PV_BGLU + cc:PV_BGLU + cc + 1]
                for tt in range(5):
                    if tt < 4:
                        tsl = slice(tt * 512, (tt + 1) * 512)
                        n = 512
                    else:
                        tsl = slice(2048, 2080)
                        n = 32
                    pa = nxt("F", psF)
                    for k in range(8):
                        P.mm(pa[:, 0:n], w[:, k, 0, :], hT[:, k, tsl], start=(k == 0), stop=(k == 7))
                    pb = nxt("F", psF)
                    for k in range(8):
                        P.mm(pb[:, 0:n], w[:, k, 1, :], hT[:, k, tsl], start=(k == 0), stop=(k == 7))
                    t_ = nxt("t", tha)
                    g_ = tga[(cnt["t"] - 1) % 2]
                    P.act(t_[:, 0:n], pb[:, 0:n], AF.Tanh, bias=hbb[:, cc:cc + 1], scale=0.5)
                    P.act(g_[:, 0:n], pa[:, 0:n], AF.Identity, bias=ba, scale=1.0)
                    if tt < 4:
                        P.stt("dve", uT[:, cc, 15 + tt * 512:15 + (tt + 1) * 512], t_[:, :], 1.0, g_[:, :],
                              ALU.add, ALU.mult)
                    else:
                        P.stt("dve", uh[:, :], t_[:, 0:32], 1.0, g_[:, 0:32], ALU.add, ALU.mult)
                        P.tt("dve", uT[:, cc, 0:15], uh[:, 0:15], hmask[:, 0:15], ALU.mult)
                        P.tt("dve", uT[:, cc, 2063:2078], uh[:, 15:30], hmask[:, 15:30], ALU.mult)

            for tt in range(4):
                for cc in range(8):
                    d_ = nxt("dg", dg)
                    P.tt("pool", d_[:, :, :], bc(identf[:, :], 1, [128, 31, 128]), bc(cw[:, cc, :], 2, [128, 31, 128]),
                         ALU.mult)
                    pc = nxt("F", psF)
                    for j in range(31):
                        P.mm(pc[:, :], d_[:, j, :], uT[:, cc, tt * 512 + j:tt * 512 + j + 512],
                             start=(j == 0), stop=(j == 30))
                    P.act(cpre[:, cc, :], pc[:, :], AF.Identity, bias=pv[:, PV_CONVB + cc:PV_CONVB + cc + 1], scale=1.0)
                pm = nxt("F", psF)
                for cc in range(8):
                    P.mm(pm[:, :], onesf[:, :], cpre[:, cc, :], start=(cc == 0), stop=(cc == 7))
                pq = nxt("F", psF)
                for cc in range(8):
                    q_ = nxt("sq", sq)
                    P.tt("pool", q_[:, :], cpre[:, cc, :], cpre[:, cc, :], ALU.mult)
                    P.mm(pq[:, :], onesf[:, :], q_[:, :], start=(cc == 0), stop=(cc == 7))
                P.act(mean_t[:, :], pm[:, :], AF.Identity, scale=1.0 / D)
                P.act(rstd_t[:, :], pq[:, :], AF.Identity, scale=1.0 / D)
                P.tt("pool", msq[:, :], mean_t[:, :], mean_t[:, :], ALU.mult)
                P.tt("dve", rstd_t[:, :], rstd_t[:, :], msq[:, :], ALU.subtract)
                P.ts("dve", rstd_t[:, :], rstd_t[:, :], GN_EPS, ALU.add)
                P.tt("pool", rstd_t[:, :], rstd_t[:, :], V(cneg.h[:, 0:1].broadcast_to([128, 512]), cneg.reg), ALU.pow)
                for cc in range(8):
                    a1 = nxt("t", tn1)
                    a2 = tn2[(cnt["t"] - 1) % 2]
                    a3 = thz[(cnt["t"] - 1) % 2]
                    P.tt("dve", a1[:, :], cpre[:, cc, :], mean_t[:, :], ALU.subtract)
                    P.tt("pool", a2[:, :], a1[:, :], rstd_t[:, :], ALU.mult)
                    P.ts("dve", a1[:, :], a2[:, :], pv[:, PV_LNW + cc:PV_LNW + cc + 1], ALU.mult,
                         pv[:, PV_LNB + cc:PV_LNB + cc + 1], ALU.add)
                    P.act(a3[:, :], a1[:, :], AF.Tanh, scale=0.5)
                    P.stt("dve", cT[:, cc, :], a3[:, :], 1.0, a1[:, :], ALU.add, ALU.mult)
                yc = nxt("yc", ycT)
                for oc in range(8):
                    py = nxt("F", psF)
                    for cc in range(8):
                        P.mm(py[:, :], wco[:, cc, oc * 128:(oc + 1) * 128], cT[:, cc, :], start=(cc == 0), stop=(cc == 7))
                    P.act(yc[:, oc, :], py[:, :], AF.Identity, bias=pv[:, PV_BCO + oc:PV_BCO + oc + 1], scale=0.5)
                P.dma("sp", yc_d.v(yc_d.h.ap()[:, tt * 512:(tt + 1) * 512].rearrange("(o p) t -> p o t", p=128)),
                      yc[:, :, :], f"ycs{(cnt['yc'] - 1) % 2}")
        P.barrier()
        if dbg:
            d = dbg_out("yc", [D, NTOK], BF16)
            P.dma("sp", d[:, :], yc_d[:, :], "dbg")
        if stop_after <= 2:
            P.emit()
            return nc, P, dbg_d

        with ExitStack() as s3:
            psF = [P.ps([128, 512], F32, f"ps3F{i}", s3) for i in range(7)]
            psT = P.ps([128, 8, 128], BF16, "ps3T", s3)
            cnt = {"F": 0, "ws": 0, "th": 0, "rt": 0}

            def nxt(key, lst):
                i = cnt[key] % len(lst)
                cnt[key] += 1
                return lst[i]

            NWS = 3
            ws = [P.sb([128, 4096], BF16, f"ws{i}", s3) for i in range(NWS)]
            ogt_t = P.sb([128, 16, 512], BF16, "ogt_t", s3)
            yc_t = P.sb([128, 8, 512], BF16, "yc_t", s3)
            thr = [P.sb([128, 512], F32, f"thr{i}", s3) for i in range(2)]
            thc = [P.sb([128, 512], F32, f"thc{i}", s3) for i in range(2)]
            t1m = [P.sb([128, 512], F32, f"t1m{i}", s3) for i in range(2)]
            t2m = [P.sb([128, 512], F32, f"t2m{i}", s3) for i in range(2)]
            mT = P.sb([128, 8, 512], BF16, "mT", s3)
            x_t = P.sb([128, 4, D], F32, "x_t", s3)
            x1_t = P.sb([128, 4, D], F32, "x1_t", s3)
            junk = P.sb([128, D], F32, "junk3", s3)
            ss2 = P.sb([128, 8], F32, "ss2", s3)
            rs2 = P.sb([128, 8], F32, "rs2", s3)
            h2 = P.sb([128, 4, D], BF16, "h2", s3)
            h2T = P.sb([128, 8, 512], BF16, "h2T", s3)
            aT = [P.sb([128, 8, 512], BF16, f"aT{i}", s3) for i in range(4)]
            rtmp = [P.sb([128, 512], F32, f"rtmp{i}", s3) for i in range(2)]
            nfw = P.sb([128, D], F32, "nfw", s3)
            P.dma("sp", nfw[:, :], nfw_d.v(nfw_d.h.ap().rearrange("(a d) -> a d", a=1).partition_broadcast(128)), "nfw")

            uses = []
            for tt in range(4):
                for oc in range(8):
                    uses.append(("g", oc))
                for half in range(2):
                    uses.append(("o", half))
                for hb in range(8):
                    uses.append(("1", hb))
                for half in range(2):
                    for hcb in range(4):
                        uses.append(("2", half, hcb))
            issued = [0]

            def issue(i):
                u = uses[i]
                s_ = ws[i % NWS]
                key = f"ws{i % NWS}"
                if u[0] == "g":
                    oc = u[1]
                    P.dma("pool", V(s_.h[:, 0:1024].rearrange("p (k n) -> p k n", k=8), s_.reg),
                          wsrc(win_d, 0, 8, 8192 + oc * 128, 128), key)
                    P.dma("pool", V(s_.h[:, 1024:2048].rearrange("p (k n) -> p k n", k=8), s_.reg),
                          wsrc(win_d, 0, 8, 9216 + oc * 128, 128), key)
                    P.dma("pool", V(s_.h[:, 2048:4096].rearrange("p (k n) -> p k n", k=16), s_.reg),
                          wsrc2(wro_d, wro2, 0, 16, oc * 128, 128), key)
                elif u[0] == "o":
                    P.dma("pool", V(s_.h[:, :].rearrange("p (k n) -> p k n", k=8), s_.reg),
                          wsrc2(wout_d, wout2, 0, 8, u[1] * 512, 512), key)
                elif u[0] == "1":
                    P.dma("pool", V(s_.h[:, :].rearrange("p (k n) -> p k n", k=8), s_.reg),
                          wsrc2(w1_d, w12, 0, 8, u[1] * 512, 512), key)
                else:
                    P.dma("pool", V(s_.h[:, :].rearrange("p (k n) -> p k n", k=8), s_.reg),
                          wsrc2(w2_d, w22, u[2] * 1024, 8, u[1] * 512, 512), key)

            ucnt = [0]

            def wget():
                i = ucnt[0]
                ucnt[0] += 1
                while issued[0] < len(uses) and issued[0] <= i + NWS - 1:
                    issue(issued[0])
                    issued[0] += 1
                return ws[i % NWS]

            for tt in range(4):
                tsl = slice(tt * 512, (tt + 1) * 512)
                P.dma("sp", x_t[:, :, :], x_d.v(x_d.h.ap()[tsl, :].rearrange("(s p) d -> p s d", p=128)), "xl")
                P.dma("sp", ogt_t[:, :, :], ogt_d.v(ogt_d.h.ap()[:, tsl].rearrange("(k p) t -> p k t", p=128)), "ogl")
                P.dma("sp", yc_t[:, :, :], yc_d.v(yc_d.h.ap()[:, tsl].rearrange("(k p) t -> p k t", p=128)), "ycl")
                for oc in range(8):
                    s_ = wget()
                    sgr = V(s_.h[:, 0:1024].rearrange("p (k n) -> p k n", k=8), s_.reg)
                    sgc = V(s_.h[:, 1024:2048].rearrange("p (k n) -> p k n", k=8), s_.reg)
                    swr = V(s_.h[:, 2048:4096].rearrange("p (k n) -> p k n", k=16), s_.reg)
                    pgr = nxt("F", psF)
                    for k in range(8):
                        P.mm(pgr[:, :], sgr[:, k, :], hT[:, k, tsl], start=(k == 0), stop=(k == 7))
                    pgc = nxt("F", psF)
                    for k in range(8):
                        P.mm(pgc[:, :], sgc[:, k, :], hT[:, k, tsl], start=(k == 0), stop=(k == 7))
                    pyr = nxt("F", psF)
                    for k in range(16):
                        P.mm(pyr[:, :], swr[:, k, :], ogt_t[:, k, :], start=(k == 0), stop=(k == 15))
                    i_ = cnt["th"] % 2
                    cnt["th"] += 1
                    P.act(thr[i_][:, :], pgr[:, :], AF.Tanh, scale=0.5)
                    P.act(thc[i_][:, :], pgc[:, :], AF.Tanh, scale=0.5)
                    P.stt("dve", t1m[i_][:, :], thr[i_][:, :], 1.0, pyr[:, :], ALU.add, ALU.mult)
                    P.stt("dve", t2m[i_][:, :], thc[i_][:, :], 1.0, yc_t[:, oc, :], ALU.add, ALU.mult)
                    P.tt("dve", mT[:, oc, :], t1m[i_][:, :], t2m[i_][:, :], ALU.add)
                P.memset("dve", ss2[:, :], 0.0)
                for half in range(2):
                    s_ = wget()
                    wv = V(s_.h[:, :].rearrange("p (k n) -> p k n", k=8), s_.reg)
                    hs = slice(half * 512, (half + 1) * 512)
                    for ts_ in range(4):
                        px = nxt("F", psF)
                        for k in range(8):
                            P.mm(px[:, :], mT[:, k, ts_ * 128:(ts_ + 1) * 128], wv[:, k, :], start=(k == 0), stop=(k == 7))
                        P.stt("dve", x1_t[:, ts_, hs], px[:, :], 0.5, x_t[:, ts_, hs], ALU.mult, ALU.add)
                for ts_ in range(4):
                    P.act(junk[:, :], x1_t[:, ts_, :], AF.Square, accum=ss2[:, ts_:ts_ + 1])
                P.ts("dve", rs2[:, 0:4], ss2[:, 0:4], 1.0 / D, ALU.mult, EPS, ALU.add)
                P.tt("pool", rs2[:, 0:4], rs2[:, 0:4], V(cneg.h[:, 0:1].broadcast_to([128, 4]), cneg.reg), ALU.pow)
                for ts_ in range(4):
                    P.ts("dve", h2[:, ts_, :], x1_t[:, ts_, :], rs2[:, ts_:ts_ + 1], ALU.mult)
                    for k in range(8):
                        P.tr(psT[:, k, :], h2[:, ts_, k * 128:(k + 1) * 128], ident[:, :])
                    P.tt("dve", h2T[:, :, ts_ * 128:(ts_ + 1) * 128], psT[:, :, :],
                         bc(pv[:, PV_N2W:PV_N2W + 8], 2, [128, 8, 128]), ALU.mult)
                for hb in range(8):
                    s_ = wget()
                    wv = V(s_.h[:, :].rearrange("p (k n) -> p k n", k=8), s_.reg)
                    for hci in range(4):
                        hc = hb * 4 + hci
                        pa = nxt("F", psF)
                        for k in range(8):
                            P.mm(pa[:, :], wv[:, k, hci * 128:(hci + 1) * 128], h2T[:, k, :], start=(k == 0), stop=(k == 7))
                        r_ = nxt("rt", rtmp)
                        P.act(r_[:, :], pa[:, :], AF.Relu)
                        P.tt("pool", aT[hc // 8][:, hc % 8, :], r_[:, :], r_[:, :], ALU.mult)
                for half in range(2):
                    hs = slice(half * 512, (half + 1) * 512)
                    pacc = [nxt("F", psF) for _ in range(4)]
                    for hcb in range(4):
                        s_ = wget()
                        wv = V(s_.h[:, :].rearrange("p (k n) -> p k n", k=8), s_.reg)
                        for ts_ in range(4):
                            for i in range(8):
                                P.mm(pacc[ts_][:, :], aT[hcb][:, i, ts_ * 128:(ts_ + 1) * 128], wv[:, i, :],
                                     start=(hcb == 0 and i == 0), stop=(hcb == 3 and i == 7))
                    for ts_ in range(4):
                        P.tt("dve", x1_t[:, ts_, hs], pacc[ts_][:, :], x1_t[:, ts_, hs], ALU.add)
                P.memset("dve", ss2[:, 4:8], 0.0)
                for ts_ in range(4):
                    P.act(junk[:, :], x1_t[:, ts_, :], AF.Square, accum=ss2[:, 4 + ts_:5 + ts_])
                P.ts("dve", rs2[:, 4:8], ss2[:, 4:8], 1.0 / D, ALU.mult, EPS, ALU.add)
                P.tt("pool", rs2[:, 4:8], rs2[:, 4:8], V(cneg.h[:, 0:1].broadcast_to([128, 4]), cneg.reg), ALU.pow)
                for ts_ in range(4):
                    P.stt("dve", x_t[:, ts_, :], x1_t[:, ts_, :], rs2[:, 4 + ts_:5 + ts_], nfw[:, :], ALU.mult, ALU.mult)
                P.dma("sp", out_d.v(out_d.h.ap()[tsl, :].rearrange("(s p) d -> p s d", p=128)), x_t[:, :, :], "outs")
        P.emit()
        return nc, P, dbg_d


def host_consts():
    p = np.arange(128, dtype=np.float32)[:, None]
    i = np.arange(128, dtype=np.float32)[None, :]
    cst = np.zeros((128, CST_N), np.float32)
    cst[:, C_P1:C_P1 + 128] = np.maximum(i - p, 0)
    cst[:, C_P2:C_P2 + 128] = np.maximum(p - i, 0)
    cst[:, C_C1:C_C1 + 128] = i + 1
    cst[:, C_C2:C_C2 + 128] = 128 - i
    cst[:, C_P] = p[:, 0]
    cst[:, C_127P] = 127 - p[:, 0]
    n = np.arange(NCH, dtype=np.float32)[None, :]
    cst[:, C_TF:C_TF + 16] = 2047 - (n * 128 + p)
    cst[:, C_TB:C_TB + 16] = n * 128 + p
    return cst


def core_inputs(inputs, c):
    b, j = divmod(c, 4)
    s0 = j * NTOK
    x = np.asarray(inputs["x"])
    m = {}
    m["x"] = np.ascontiguousarray(x[b, s0:s0 + NTOK])
    xh = np.zeros((32, D), np.float32)
    hm = np.zeros((128, 32), np.float32)
    if j > 0:
        xh[0:15] = x[b, s0 - 15:s0]
        hm[:, 0:15] = 1.0
    if j < 3:
        xh[15:30] = x[b, s0 + NTOK:s0 + NTOK + 15]
        hm[:, 15:30] = 1.0
    m["xh"] = xh
    m["hmask"] = hm
    inv_freq = (np.float32(10000.0) ** (-np.arange(0, 128, 2, dtype=np.float32) / np.float32(128))).astype(np.float32)
    pos = (s0 + np.arange(NTOK, dtype=np.float32))
    ang = (pos[:, None] * inv_freq[None, :]).astype(np.float32)
    cos = np.cos(ang.astype(np.float64)).astype(np.float32).reshape(NCH, 128, 64).transpose(1, 0, 2)
    sin = np.sin(ang.astype(np.float64)).astype(np.float32).reshape(NCH, 128, 64).transpose(1, 0, 2)
    m["rope"] = np.ascontiguousarray(np.stack([cos, sin], 1).reshape(128, 2 * NCH * 64))
    m["cst"] = host_consts()
    cce = np.zeros((128, 8), np.float32)
    ccm = np.zeros((128, 8), np.float32)
    for r in range(4):
        if r < j:
            cce[:, r] = NTOK * (j - 1 - r)
            ccm[:, r] = 1.0
        if r > j:
            cce[:, 4 + r] = NTOK * (r - j - 1)
            ccm[:, 4 + r] = 1.0
    m["cce"] = cce
    m["ccm"] = ccm
    for k in ("norm1_w", "w_in", "ret_decay_raw", "ret_gn_w", "w_ret_o", "b_glu", "conv_w", "conv_b",
              "conv_ln_w", "conv_ln_b", "w_conv_o", "b_conv_o", "w_out", "norm2_w", "w_mlp1", "w_mlp2",
              "norm_f_w"):
        m[k] = np.ascontiguousarray(np.asarray(inputs[k], dtype=np.float32))
    return m


def kernel(**inputs):
    nc, P, _ = build()
    in_maps = [core_inputs(inputs, c) for c in range(NCORES)]
    res = run_bass_kernel_spmd(nc, in_maps, core_ids=list(range(NCORES)))
    out = np.zeros((2, 8192, D), np.float32)
    for c in range(NCORES):
        b, j = divmod(c, 4)
        out[b, j * NTOK:(j + 1) * NTOK] = res.results[c]["out"]
    return out
```

```python
import math
from contextlib import ExitStack

import numpy as np
import concourse.bass as bass
import concourse.mybir as mybir
from concourse.bass_utils import run_bass_kernel_spmd

F32 = mybir.dt.float32
BF16 = mybir.dt.bfloat16
AF = mybir.ActivationFunctionType
ALU = mybir.AluOpType
AX = mybir.AxisListType

NCORES = 8
D = 1024
NTOK = 2048
NCH = 16
NH = 8
KS = 128 ** -0.5
LNKS = math.log(KS)
EPS = 1e-6
GN_EPS = 1e-5


class Reg:
    __slots__ = ("name", "w", "r", "dw", "dr")

    def __init__(self, name):
        self.name = name
        self.w = {}
        self.r = {}
        self.dw = {}
        self.dr = {}


class V:
    __slots__ = ("ap", "reg")

    def __init__(self, ap, reg):
        self.ap = ap
        self.reg = reg

    def __getitem__(self, idx):
        return V(self.ap[idx], self.reg)


class T:
    def __init__(self, h, name):
        self.h = h
        self.reg = Reg(name)

    def __getitem__(self, idx):
        return V(self.h[idx], self.reg)

    def v(self, ap):
        return V(ap, self.reg)

    def carve(self, idx, name):
        return V(self.h[idx], Reg(name))


class Prog:
    ENGS = ("pe", "act", "dve", "pool", "sp")

    def __init__(self, nc):
        self.nc = nc
        self.ops = {e: [] for e in self.ENGS}
        self.dma_cnt = {}
        self.n_t = 0

    def sb(self, shape, dt, name, stack=None):
        self.n_t += 1
        name = f"{name}_{self.n_t}"
        if stack is None:
            h = self.nc.alloc_sbuf_tensor(name, list(shape), dt)
        else:
            h = stack.enter_context(self.nc.sbuf_tensor(name, list(shape), dt))
        return T(h, name)

    def ps(self, shape, dt, name, stack):
        self.n_t += 1
        name = f"{name}_{self.n_t}"
        h = stack.enter_context(self.nc.psum_tensor(name, list(shape), dt))
        return T(h, name)

    def dram(self, shape, dt, name, kind="Internal"):
        return T(self.nc.dram_tensor(name, list(shape), dt, kind=kind), name)

    def op(self, eng, emit, reads=(), writes=(), dma_key=None, inc=16, part=False):
        lst = self.ops[eng]
        idx = len(lst)
        is_dma = dma_key is not None
        de, dd = {}, {}
        for r in reads:
            r = r.reg
            for e, i in r.w.items():
                if e != eng or eng != "pe" or is_dma:
                    if de.get(e, -1) < i:
                        de[e] = i
            for k, c in r.dw.items():
                if dd.get(k, -1) < c:
                    dd[k] = c
        for w in writes:
            w = w.reg
            for e, i in list(w.r.items()) + list(w.w.items()):
                if e != eng or is_dma or eng != "pe":
                    if de.get(e, -1) < i:
                        de[e] = i
            for k, c in list(w.dr.items()) + list(w.dw.items()):
                if part and k == dma_key and k in w.dw:
                    continue
                if dd.get(k, -1) < c:
                    dd[k] = c
        cnt = None
        if is_dma:
            cnt = self.dma_cnt.get(dma_key, 0) + inc
            self.dma_cnt[dma_key] = cnt
        for r in reads:
            r = r.reg
            if is_dma:
                r.dr[dma_key] = cnt
            else:
                r.r[eng] = idx
        for w in writes:
            w = w.reg
            w.r.clear()
            w.dr.clear()
            w.w.clear()
            w.dw.clear()
            if is_dma:
                w.dw[dma_key] = cnt
            else:
                w.w[eng] = idx
        lst.append(dict(emit=emit, de=de, dd=dd, dma_key=dma_key, inc=inc))
        return idx

    def barrier(self):
        last = {}
        for e in self.ENGS:
            nd = [i for i, o in enumerate(self.ops[e]) if o["dma_key"] is None and o["emit"] is not None]
            if nd:
                last[e] = nd[-1]
        for e in self.ENGS:
            de = {e2: i for e2, i in last.items() if e2 != e}
            self.ops[e].append(dict(emit=None, de=de, dd=dict(self.dma_cnt), dma_key=None, inc=0))

    def emit(self):
        nc = self.nc
        sig = {e: set() for e in self.ENGS}
        for e in self.ENGS:
            for o in self.ops[e]:
                for e2, i2 in o["de"].items():
                    sig[e2].add(i2)
        last = {}
        for e in self.ENGS:
            if e != "sp":
                nd = [i for i, o in enumerate(self.ops[e]) if o["dma_key"] is None and o["emit"] is not None]
                if nd:
                    sig[e].add(nd[-1])
                    last[e] = nd[-1]
        cnts = {}
        for e in self.ENGS:
            c = 0
            m = {}
            for i in range(len(self.ops[e])):
                if i in sig[e]:
                    c += 1
                    m[i] = c
            cnts[e] = m
        esem = {e: nc.alloc_semaphore(f"sem_{e}") for e in self.ENGS}
        dsem = {k: nc.alloc_semaphore(f"dsem_{k}") for k in self.dma_cnt}
        self.stats = {e: len(self.ops[e]) for e in self.ENGS}
        self.stats["nsem"] = len(esem) + len(dsem)
        prog = self

        def replay(e, eng):
            seen_e = {}
            seen_d = {}
            nwait = 0
            for i, o in enumerate(prog.ops[e]):
                for e2, i2 in o["de"].items():
                    v = cnts[e2][i2]
                    if seen_e.get(e2, 0) < v:
                        eng.wait_ge(esem[e2], v)
                        seen_e[e2] = v
                        nwait += 1
                for k, c in o["dd"].items():
                    if seen_d.get(k, 0) < c:
                        eng.wait_ge(dsem[k], c)
                        seen_d[k] = c
                        nwait += 1
                if o["emit"] is None:
                    continue
                ins = o["emit"](eng)
                if o["dma_key"] is not None:
                    ins.then_inc(dsem[o["dma_key"]], o["inc"])
                    assert i not in sig[e]
                elif i in sig[e]:
                    ins.then_inc(esem[e], 1)
            if e == "sp":
                for e2, i2 in last.items():
                    eng.wait_ge(esem[e2], cnts[e2][i2])
                for k, c in prog.dma_cnt.items():
                    eng.wait_ge(dsem[k], c)
            prog.stats["w_" + e] = nwait

        with nc.Block() as block:
            @block.tensor
            def _(eng):
                replay("pe", eng)

            @block.scalar
            def _(eng):
                replay("act", eng)

            @block.vector
            def _(eng):
                replay("dve", eng)

            @block.gpsimd
            def _(eng):
                replay("pool", eng)

            @block.sync
            def _(eng):
                replay("sp", eng)

    def dma(self, q, out, in_, key, part=False, **kw):
        return self.op(q, lambda e: e.dma_start(out=out.ap, in_=in_.ap, **kw),
                       reads=[in_], writes=[out], dma_key=key, part=part)

    def mm(self, out, lhsT, rhs, start=True, stop=True):
        return self.op("pe", lambda e: e.matmul(out.ap, lhsT.ap, rhs.ap, start=start, stop=stop),
                       reads=[lhsT, rhs], writes=[out])

    def tr(self, out, in_, ident):
        return self.op("pe", lambda e: e.transpose(out.ap, in_.ap, ident.ap),
                       reads=[in_, ident], writes=[out])

    def act(self, out, in_, func, bias=None, scale=None, accum=None):
        kw = {}
        rd = [in_]
        wr = [out]
        if bias is not None:
            if isinstance(bias, V):
                kw["bias"] = bias.ap
                rd.append(bias)
            else:
                kw["bias"] = bias
        if scale is not None:
            if isinstance(scale, V):
                kw["scale"] = scale.ap
                rd.append(scale)
            else:
                kw["scale"] = scale
        if accum is not None:
            kw["accum_out"] = accum.ap
            wr.append(accum)
        return self.op("act", lambda e: e.activation(out.ap, in_.ap, func, **kw), reads=rd, writes=wr)

    def tt(self, eng, out, a, b, op):
        return self.op(eng, lambda e: e.tensor_tensor(out.ap, a.ap, b.ap, op), reads=[a, b], writes=[out])

    def ts(self, eng, out, a, s1, op0, s2=None, op1=None):
        rd = [a]
        s1a = s1.ap if isinstance(s1, V) else s1
        s2a = s2.ap if isinstance(s2, V) else s2
        if isinstance(s1, V):
            rd.append(s1)
        if isinstance(s2, V):
            rd.append(s2)
        kw = {}
        if op1 is not None:
            kw["op1"] = op1
        return self.op(eng, lambda e: e.tensor_scalar(out.ap, a.ap, s1a, s2a, op0, **kw), reads=rd, writes=[out])

    def stt(self, eng, out, a, s, b, op0, op1):
        rd = [a, b]
        sa = s.ap if isinstance(s, V) else s
        if isinstance(s, V):
            rd.append(s)
        return self.op(eng, lambda e: e.scalar_tensor_tensor(out.ap, a.ap, sa, b.ap, op0, op1),
                       reads=rd, writes=[out])

    def red(self, eng, out, in_, op=None):
        op = op or ALU.add
        return self.op(eng, lambda e: e.tensor_reduce(out.ap, in_.ap, AX.X, op), reads=[in_], writes=[out])

    def cp(self, eng, out, in_):
        if eng == "act":
            return self.op(eng, lambda e: e.copy(out.ap, in_.ap), reads=[in_], writes=[out])
        return self.op(eng, lambda e: e.tensor_copy(out.ap, in_.ap), reads=[in_], writes=[out])

    def memset(self, eng, out, val):
        return self.op(eng, lambda e: e.memset(out.ap, val), writes=[out])


def bc(v, axis, shape):
    return V(v.ap.unsqueeze(axis).broadcast_to(list(shape)), v.reg)


C_P1, C_P2, C_C1, C_C2 = 0, 128, 256, 384
C_P, C_127P, C_TF, C_TB = 512, 513, 514, 530
CST_N = 546
PV_N1W, PV_BGLU, PV_CONVB, PV_LNW, PV_LNB, PV_BCO, PV_N2W = 0, 8, 24, 32, 40, 48, 56
PV_GNW = 64
PV_ROWS = 80


def build(stop_after=99, dbg=False, nheads=NH, sub=9):
    nc = bass.Bass("TRN2", target_bir_lowering=False)
    P = Prog(nc)
    ein = "ExternalInput"
    x_d = P.dram([NTOK, D], F32, "x", ein)
    xh_d = P.dram([32, D], F32, "xh", ein)
    hmask_d = P.dram([128, 32], F32, "hmask", ein)
    rope_d = P.dram([128, 2 * NCH * 64], F32, "rope", ein)
    cst_d = P.dram([128, CST_N], F32, "cst", ein)
    cce_d = P.dram([128, 8], F32, "cce", ein)
    ccm_d = P.dram([128, 8], F32, "ccm", ein)
    n1w_d = P.dram([1, D], F32, "norm1_w", ein)
    win_d = P.dram([1, D, 10240], F32, "w_in", ein)
    raw_d = P.dram([1, 2, NH], F32, "ret_decay_raw", ein)
    gnw_d = P.dram([1, 2048], F32, "ret_gn_w", ein)
    wro_d = P.dram([1, 2048, D], F32, "w_ret_o", ein)
    bglu_d = P.dram([1, 2048], F32, "b_glu", ein)
    cw_d = P.dram([1, 31, D], F32, "conv_w", ein)
    cb_d = P.dram([1, D], F32, "conv_b", ein)
    lnw_d = P.dram([1, D], F32, "conv_ln_w", ein)
    lnb_d = P.dram([1, D], F32, "conv_ln_b", ein)
    wco_d = P.dram([1, D, D], F32, "w_conv_o", ein)
    bco_d = P.dram([1, D], F32, "b_conv_o", ein)
    wout_d = P.dram([1, D, D], F32, "w_out", ein)
    n2w_d = P.dram([1, D], F32, "norm2_w", ein)
    w1_d = P.dram([1, D, 4096], F32, "w_mlp1", ein)
    w2_d = P.dram([1, 4096, D], F32, "w_mlp2", ein)
    nfw_d = P.dram([D], F32, "norm_f_w", ein)
    out_d = P.dram([NTOK, D], F32, "out", "ExternalOutput")
    ogt_d = P.dram([2048, NTOK], BF16, "ogt_scr")
    yc_d = P.dram([D, NTOK], BF16, "yc_scr")
    ain_d = [P.dram([256, 256], F32, f"ain{i}") for i in range(2)]
    aout_d = [P.dram([4 * 256, 256], F32, f"aout{i}") for i in range(2)]
    dbg_d = {}

    def dbg_out(name, shape, dt=F32):
        dbg_d[name] = P.dram(shape, dt, "dbg_" + name, "ExternalOutput")
        return dbg_d[name]

    win = win_d.h.ap()[0]

    def wsrc(t, r0, nk, c0, ncol, ap2=None):
        a = ap2 if ap2 is not None else win
        return V(a[r0:r0 + nk * 128, c0:c0 + ncol].rearrange("(k p) n -> p k n", p=128), t.reg)

    wg_s = [P.dram([128, 4096], BF16, f"wg_s{i}") for i in range(8)]
    wo_s = [P.dram([128, 4096], BF16, f"wo_s{i}") for i in range(2)]
    w1_s = [P.dram([128, 4096], BF16, f"w1_s{i}") for i in range(8)]
    w2_s = [P.dram([128, 4096], BF16, f"w2_s{i}") for i in range(8)]
    pc_jobs = []

    def pcj(dst_t, c0, nk, src_ap2, r0, col0, ncol):
        dst = V(dst_t.h.ap()[:, c0:c0 + nk * ncol].rearrange("p (k n) -> p k n", k=nk), dst_t.reg)
        src = V(src_ap2[r0:r0 + nk * 128, col0:col0 + ncol].rearrange("(k p) n -> p k n", p=128), Reg("wsrc"))
        pc_jobs.append((dst, src))

    for oc in range(8):
        pcj(wg_s[oc], 0, 8, win, 0, 8192 + oc * 128, 128)
        pcj(wg_s[oc], 1024, 8, win, 0, 9216 + oc * 128, 128)
        pcj(wg_s[oc], 2048, 16, wro_d.h.ap()[0], 0, oc * 128, 128)
    for half in range(2):
        pcj(wo_s[half], 0, 8, wout_d.h.ap()[0], 0, half * 512, 512)
    for hb in range(8):
        pcj(w1_s[hb], 0, 8, w1_d.h.ap()[0], 0, hb * 512, 512)
    for half in range(2):
        for hcb in range(4):
            pcj(w2_s[half * 4 + hcb], 0, 8, w2_d.h.ap()[0], hcb * 1024, half * 512, 512)

    def precast(n):
        for _ in range(n):
            if pc_jobs:
                dst, src = pc_jobs.pop(0)
                P.dma("pool", dst, src, "pc", part=True)

    top = ExitStack()
    with top:
        hT = P.sb([128, 8, 2080], BF16, "hT", top)
        ident = P.sb([128, 128], BF16, "ident", top)
        identf = P.sb([128, 128], F32, "identf", top)
        pv = P.sb([128, PV_ROWS], F32, "pv", top)
        cneg = P.sb([128, 1], F32, "cneg", top)

        P.memset("pool", identf[:, :], 0.0)
        P.op("pool", lambda e: e.affine_select(identf.h[:, :], identf.h[:, :], pattern=[[-1, 128]],
                                               compare_op=ALU.not_equal, fill=1.0, base=0,
                                               channel_multiplier=1),
             reads=[identf], writes=[identf])
        P.cp("dve", ident[:, :], identf[:, :])
        P.memset("pool", cneg[:, :], -0.5)

        with ExitStack() as s0:
            pvr = P.sb([PV_ROWS, 128], F32, "pvr", s0)
            for (row, dd_, n) in ((PV_N1W, n1w_d, 8), (PV_BGLU, bglu_d, 16), (PV_CONVB, cb_d, 8),
                                  (PV_LNW, lnw_d, 8), (PV_LNB, lnb_d, 8), (PV_BCO, bco_d, 8),
                                  (PV_N2W, n2w_d, 8), (PV_GNW, gnw_d, 16)):
                P.dma("sp", pvr[row:row + n, :],
                      dd_.v(dd_.h.ap().rearrange("a (k p) -> (a k) p", p=128)), "par")
            ptp = P.ps([128, 512], F32, "ptp", s0)
            P.tr(ptp[:, 0:PV_ROWS], pvr[:, :], identf[0:PV_ROWS, 0:PV_ROWS])
            P.cp("dve", pv[:, :], ptp[:, 0:PV_ROWS])

            xt = [P.sb([128, 4, D], F32, f"xt{i}", s0) for i in range(2)]
            xsb = [P.sb([128, 4, D], BF16, f"xsb{i}", s0) for i in range(2)]
            junk = P.sb([128, D], F32, "junk", s0)
            ss = P.sb([128, 20], F32, "ss", s0)
            rstd = P.sb([128, 20], F32, "rstd", s0)
            ptr = [P.ps([128, 8, 128], BF16, f"ptr{i}", s0) for i in range(2)]
            P.memset("dve", ss[:, :], 0.0)
            P.memset("pool", xt[0][:, :, :], 0.0)
            for g in range(5):
                b = g % 2
                if g < 4:
                    P.dma("sp", xt[b][:, :, :],
                          x_d.v(x_d.h.ap()[g * 512:(g + 1) * 512, :].rearrange("(c p) d -> p c d", p=128)),
                          f"x{b}")
                    nsub = 4
                else:
                    P.dma("sp", xt[b][0:32, 0, :], xh_d[:, :], f"x{b}")
                    nsub = 1
                for c in range(nsub):
                    P.act(junk[:, :], xt[b][:, c, :], AF.Square, accum=ss[:, g * 4 + c:g * 4 + c + 1])
                sl = slice(g * 4, g * 4 + nsub)
                P.ts("dve", rstd[:, sl], ss[:, sl], 1.0 / D, ALU.mult, EPS, ALU.add)
                P.tt("pool", rstd[:, sl], rstd[:, sl], V(cneg.h[:, 0:1].broadcast_to([128, nsub]), cneg.reg), ALU.pow)
                for c in range(nsub):
                    P.ts("dve", xsb[b][:, c, :], xt[b][:, c, :], rstd[:, g * 4 + c:g * 4 + c + 1], ALU.mult)
                    pt = ptr[c % 2]
                    for k in range(8):
                        P.tr(pt[:, k, :], xsb[b][:, c, k * 128:(k + 1) * 128], ident[:, :])
                    n1 = bc(pv[:, PV_N1W:PV_N1W + 8], 2, [128, 8, 128])
                    if g < 4:
                        col = (g * 4 + c) * 128
                        P.tt("dve", hT[:, :, col:col + 128], pt[:, :, :], n1, ALU.mult)
                    else:
                        n1h = bc(pv[:, PV_N1W:PV_N1W + 8], 2, [128, 8, 32])
                        P.tt("dve", hT[:, :, 2048:2080], pt[:, :, 0:32], n1h, ALU.mult)
        P.barrier()
        if dbg:
            d = dbg_out("hT", [128, 8 * 2080], BF16)
            P.dma("sp", d.v(d.h.ap().rearrange("p (k t) -> p k t", k=8)), hT[:, :, :], "dbg")
        if stop_after <= 0:
            P.emit()
            return nc, P, dbg_d

        with ExitStack() as s1:
            cst = P.sb([128, CST_N], F32, "cst", s1)
            P.dma("sp", cst[:, :], cst_d[:, :], "par")
            rope = P.sb([128, 2, NCH, 64], F32, "rope", s1)
            P.dma("sp", rope[:, :, :, :], rope_d.v(rope_d.h.ap().rearrange("p (a n f) -> p a n f", a=2, n=NCH)), "par")
            gpp = P.sb([128, 16], F32, "gpp", s1)
            P.ts("dve", gpp[:, :], pv[:, PV_GNW:PV_GNW + 16], 0.5, ALU.mult)
            lg = P.sb([128, 16], F32, "lg", s1)
            P.dma("sp", lg[:, :], raw_d.v(raw_d.h.ap().rearrange("a d h -> a (d h)").partition_broadcast(128)), "par")
            cce = P.sb([128, 8], F32, "cce", s1)
            ccm = P.sb([128, 8], F32, "ccm", s1)
            P.dma("sp", cce[:, :], cce_d[:, :], "par")
            P.dma("sp", ccm[:, :], ccm_d[:, :], "par")
            for t_ in (cst, rope, lg, cce, ccm):
                t_.reg.dw["par"] = P.dma_cnt["par"]

            P.act(lg[:, :], lg[:, :], AF.Exp)
            P.ts("dve", lg[:, :], lg[:, :], -1.0, ALU.mult)
            DT = P.sb([128, 128], F32, "DT", s1)
            XF = P.sb([128, 128], F32, "XF", s1)
            XB = P.sb([128, 128], F32, "XB", s1)
            zf = P.sb([128, NH], F32, "zf", s1)
            zb = P.sb([128, NH], F32, "zb", s1)
            zF = P.sb([128, NH, NCH], F32, "zF", s1)
            zB = P.sb([128, NH, NCH], F32, "zB", s1)
            g128 = P.sb([128, 16], F32, "g128", s1)
            cc = P.sb([128, 2, 4, NH], F32, "cc", s1)
            tmpc = P.sb([128, 128], F32, "tmpc", s1)
            for h in range(NH):
                lf = lg[:, h:h + 1]
                lb = lg[:, 8 + h:9 + h]
                P.ts("dve", tmpc[:, 0:16], cst[:, C_TF:C_TF + 16], lf, ALU.mult, LNKS, ALU.add)
                P.act(zF[:, h, :], tmpc[:, 0:16], AF.Exp)
                P.ts("dve", tmpc[:, 16:32], cst[:, C_TB:C_TB + 16], lb, ALU.mult, LNKS, ALU.add)
                P.act(zB[:, h, :], tmpc[:, 16:32], AF.Exp)
                P.ts("dve", tmpc[:, 32:33], cst[:, C_127P:C_127P + 1], lf, ALU.mult, LNKS, ALU.add)
                P.act(zf[:, h:h + 1], tmpc[:, 32:33], AF.Exp)
                P.ts("dve", tmpc[:, 33:34], cst[:, C_P:C_P + 1], lb, ALU.mult, LNKS, ALU.add)
                P.act(zb[:, h:h + 1], tmpc[:, 33:34], AF.Exp)
            P.act(g128[:, :], lg[:, :], AF.Exp, scale=128.0)
            for d_ in range(2):
                for r in range(4):
                    P.ts("dve", tmpc[:, 64:72], lg[:, d_ * 8:d_ * 8 + 8], cce[:, d_ * 4 + r:d_ * 4 + r + 1], ALU.mult)
                    P.act(tmpc[:, 72:80], tmpc[:, 64:72], AF.Exp)
                    P.ts("dve", cc[:, d_, r, :], tmpc[:, 72:80], ccm[:, d_ * 4 + r:d_ * 4 + r + 1], ALU.mult)

            psF = [P.ps([128, 512], F32, f"psF{i}", s1) for i in range(6)]
            psT = [P.ps([128, 1024], BF16, f"psT{i}", s1) for i in range(2)]

            wb = [P.sb([128, 8, 768], BF16, f"wb{i}", s1) for i in range(2)]
            qkT = P.sb([128, 2, NTOK], BF16, "qkT", s1)
            qfT = P.sb([128, NTOK], BF16, "qfT", s1)
            qbT = P.sb([128, NTOK], BF16, "qbT", s1)
            kfb = P.sb([128, 2, NCH, 128], BF16, "kfb", s1)
            vb = P.sb([128, NCH, 256], BF16, "vb", s1)
            sg = P.sb([128, NCH, 256], BF16, "sg", s1)
            kFB = P.sb([128, 2, NCH, 128], BF16, "kFB", s1)
            rawqk = [P.sb([128, 4, 2, 2, 64], F32, f"rawqk{i}", s1) for i in range(2)]
            th = [P.sb([128, 256], F32, f"th{i}", s1) for i in range(2)]
            ra = P.sb([128, 4, 2, 64], F32, "ra", s1)
            rb = P.sb([128, 4, 2, 64], F32, "rb", s1)
            rc = P.sb([128, 4, 2, 64], F32, "rc", s1)
            rd_ = P.sb([128, 4, 2, 64], F32, "rd", s1)
            rot = [P.sb([128, 4, 2, 2, 64], BF16, f"rot{i}", s1) for i in range(2)]
            AS = P.sb([128, 2, 256], F32, "AS", s1)
            agin = P.sb([128, 4, 2, 256], F32, "agin", s1)
            Sst = P.sb([128, 2, 256], F32, "Sst", s1)
            Rf32 = [P.sb([128, 2, 256], F32, f"Rf32_{i}", s1) for i in range(2)]
            Rb16 = [[P.sb([128, 256], BF16, f"Rb16_{d_}_{n}", s1) for n in range(NCH)] for d_ in range(2)]
            sd = [P.sb([128, 4, 128], BF16, f"sd{i}", s1) for i in range(2)]
            osb = [P.sb([128, 4, 256], F32, f"osb{i}", s1) for i in range(2)]
            stats = [P.sb([128, 20], F32, f"stats{i}", s1) for i in range(2)]
            junk256 = P.sb([128, 256], F32, "junk256", s1)
            og1 = P.sb([128, 4, 256], F32, "og1", s1)
            ogb = [P.sb([128, 4, 256], BF16, f"ogb{i}", s1) for i in range(2)]
            ogT = P.sb([128, 2, NTOK], BF16, "ogT", s1)

            def head_consts(h):
                lf = lg[:, h:h + 1]
                lb = lg[:, 8 + h:9 + h]
                P.ts("dve", tmpc[:, :], cst[:, C_P1:C_P1 + 128], lf, ALU.mult, LNKS, ALU.add)
                P.stt("dve", tmpc[:, :], cst[:, C_P2:C_P2 + 128], lb, tmpc[:, :], ALU.mult, ALU.add)
                P.act(DT[:, :], tmpc[:, :], AF.Exp)
                P.act(XF[:, :], cst[:, C_C1:C_C1 + 128], AF.Exp, scale=lf)
                P.act(XB[:, :], cst[:, C_C2:C_C2 + 128], AF.Exp, scale=lb)

            cnt = {"F": 0, "T": 0, "sd": 0, "rq": 0, "ro": 0, "ob": 0, "og": 0}

            def nxt(key, lst):
                i = cnt[key] % len(lst)
                cnt[key] += 1
                return lst[i]

            def load_w(h):
                w = wb[h % 2]
                k_ = f"w{h % 2}"
                P.dma("pool", w[:, :, 0:128], wsrc(win_d, 0, 8, h * 128, 128), k_)
                P.dma("pool", w[:, :, 128:256], wsrc(win_d, 0, 8, 1024 + h * 128, 128), k_)
                P.dma("pool", w[:, :, 256:512], wsrc(win_d, 0, 8, 2048 + h * 256, 256), k_)
                P.dma("pool", w[:, :, 512:768], wsrc(win_d, 0, 8, 4096 + h * 256, 256), k_)

            def proj(h):
                w = wb[h % 2]
                ros = {}
                for cg in range(4):
                    rq = nxt("rq", rawqk)
                    ro = nxt("ro", rot)
                    ros[cg] = ro
                    for ci in range(4):
                        c = cg * 4 + ci
                        pa = nxt("F", psF)
                        for k in range(8):
                            P.mm(pa[:, :], hT[:, k, c * 128:(c + 1) * 128], w[:, k, 0:512],
                                 start=(k == 0), stop=(k == 7))
                        pb = nxt("F", psF)
                        for k in range(8):
                            P.mm(pb[:, 0:256], hT[:, k, c * 128:(c + 1) * 128], w[:, k, 512:768],
                                 start=(k == 0), stop=(k == 7))
                        P.cp("act", V(rq.h[:, ci].rearrange("p a b f -> p (a b f)"), rq.reg), pa[:, 0:256])
                        P.cp("act", vb[:, c, :], pa[:, 256:512])
                        t_ = th[c % 2]
                        P.act(t_[:, :], pb[:, 0:256], AF.Tanh, scale=0.5)
                        P.stt("dve", sg[:, c, :], t_[:, :], 1.0, pb[:, 0:256], ALU.add, ALU.mult)
                    if cg > 0:
                        proj_tr(h, cg - 1, ros[cg - 1])
                    cos4 = bc(rope[:, 0, cg * 4:(cg + 1) * 4, :], 2, [128, 4, 2, 64])
                    sin4 = bc(rope[:, 1, cg * 4:(cg + 1) * 4, :], 2, [128, 4, 2, 64])
                    t1 = rq[:, :, :, 0, :]
                    t2 = rq[:, :, :, 1, :]
                    P.tt("dve", ra[:, :, :, :], t1, cos4, ALU.mult)
                    P.tt("dve", rb[:, :, :, :], t2, sin4, ALU.mult)
                    P.tt("dve", ro[:, :, :, 0, :], ra[:, :, :, :], rb[:, :, :, :], ALU.subtract)
                    P.tt("pool", rc[:, :, :, :], t2, cos4, ALU.mult)
                    P.tt("pool", rd_[:, :, :, :], t1, sin4, ALU.mult)
                    P.tt("pool", ro[:, :, :, 1, :], rc[:, :, :, :], rd_[:, :, :, :], ALU.add)
                    krot = V(ro.h[:, :, 1].rearrange("p c a f -> p c (a f)"), ro.reg)
                    csl = slice(cg * 4, (cg + 1) * 4)
                    P.ts("dve", kfb[:, 0, csl, :], krot, zf[:, h:h + 1], ALU.mult)
                    P.ts("dve", kfb[:, 1, csl, :], krot, zb[:, h:h + 1], ALU.mult)
                    P.tt("pool", kFB[:, 0, csl, :], krot, bc(zF[:, h, csl], 2, [128, 4, 128]), ALU.mult)
                    P.tt("pool", kFB[:, 1, csl, :], krot, bc(zB[:, h, csl], 2, [128, 4, 128]), ALU.mult)
                proj_tr(h, 3, ros[3])

            def proj_tr(h, cg, ro):
                pt = nxt("T", psT)
                ptv = V(pt.h[:, :].rearrange("p (c a f) -> p c a f", c=4, a=2), pt.reg)
                for ci in range(4):
                    for a in range(2):
                        P.tr(ptv[:, ci, a, :], V(ro.h[:, ci, a].rearrange("p a f -> p (a f)"), ro.reg),
                             ident[:, :])
                tsl = slice(cg * 512, (cg + 1) * 512)
                P.cp("act", V(qkT.h[:, :, tsl].rearrange("p a (c f) -> p c a f", c=4), qkT.reg), ptv[:, :, :, :])
                q4 = V(qkT.h[:, 0, tsl].rearrange("p (c f) -> p c f", c=4), qkT.reg)
                P.tt("pool", V(qfT.h[:, tsl].rearrange("p (c f) -> p c f", c=4), qfT.reg), q4,
                     bc(XF[:, :], 1, [128, 4, 128]), ALU.mult)
                P.tt("pool", V(qbT.h[:, tsl].rearrange("p (c f) -> p c f", c=4), qbT.reg), q4,
                     bc(XB[:, :], 1, [128, 4, 128]), ALU.mult)

            def phaseA(h):
                s = h % 2
                pu = nxt("F", psF)
                for d_ in range(2):
                    for c in range(NCH):
                        P.mm(pu[:, d_ * 256:(d_ + 1) * 256], kFB[:, d_, c, :], vb[:, c, :],
                             start=(c == 0), stop=(c == NCH - 1))
                P.cp("act", V(AS.h[:, :, :].rearrange("p d v -> p (d v)"), AS.reg), pu[:, :])
                P.dma("sp", ain_d[s].v(ain_d[s].h.ap().rearrange("(d p) v -> p d v", p=128)), AS[:, :, :], f"ain{s}")
                P.op("pool", lambda e: e.collective_compute(
                    "AllGather", ALU.bypass, replica_groups=[[0, 1, 2, 3], [4, 5, 6, 7]],
                    ins=[ain_d[s].h.ap().opt()], outs=[aout_d[s].h.ap().opt()]),
                    reads=[ain_d[s]], writes=[aout_d[s]], dma_key=f"cc{s}", inc=1)
                P.dma("sp", agin[:, :, :, :],
                      aout_d[s].v(aout_d[s].h.ap().rearrange("(r d p) v -> p r d v", p=128, d=2)), "agl")

            def attn(h):
                for d_ in range(2):
                    P.ts("dve", Sst[:, d_, :], agin[:, 0, d_, :], cc[:, d_, 0, h:h + 1], ALU.mult)
                    for r in range(1, 4):
                        P.stt("dve", Sst[:, d_, :], agin[:, r, d_, :], cc[:, d_, r, h:h + 1], Sst[:, d_, :],
                              ALU.mult, ALU.add)
                cur = Sst
                P.cp("pool", Rb16[0][0][:, :], Sst[:, 0, :])
                P.cp("pool", Rb16[1][NCH - 1][:, :], Sst[:, 1, :])
                for step in range(NCH - 1):
                    pu = nxt("F", psF)
                    nw = Rf32[step % 2]
                    for d_ in range(2):
                        n = step if d_ == 0 else NCH - 1 - step
                        P.mm(pu[:, d_ * 256:(d_ + 1) * 256], kfb[:, d_, n, :], vb[:, n, :])
                    for d_ in range(2):
                        n = step if d_ == 0 else NCH - 1 - step
                        nn = n + 1 if d_ == 0 else n - 1
                        P.stt("dve", nw[:, d_, :], cur[:, d_, :], g128[:, d_ * 8 + h:d_ * 8 + h + 1],
                              pu[:, d_ * 256:(d_ + 1) * 256], ALU.mult, ALU.add)
                        P.cp("pool", Rb16[d_][nn][:, :], nw[:, d_, :])
                    cur = nw
                order = sorted(range(NCH), key=lambda n: (max(n, NCH - 1 - n), n))
                pend = None
                for gi in range(4):
                    grp = order[gi * 4:(gi + 1) * 4]
                    pS = nxt("F", psF)
                    for i, n in enumerate(grp):
                        tsl = slice(n * 128, (n + 1) * 128)
                        P.mm(pS[:, i * 128:(i + 1) * 128], qkT[:, 1, tsl], qkT[:, 0, tsl])
                    sd4 = nxt("sd", sd)
                    P.tt("dve", sd4[:, :, :], V(pS.h[:, :].rearrange("p (c f) -> p c f", c=4), pS.reg),
                         bc(DT[:, :], 1, [128, 4, 128]), ALU.mult)
                    ob = nxt("ob", osb)
                    stt_ = stats[gi % 2]
                    P.memset("dve", stt_[:, 0:8], 0.0)
                    for pr in range(2):
                        pO = nxt("F", psF)
                        for i2 in range(2):
                            i = pr * 2 + i2
                            n = grp[i]
                            tsl = slice(n * 128, (n + 1) * 128)
                            po = pO[:, i2 * 256:(i2 + 1) * 256]
                            P.mm(po, sd4[:, i, :], vb[:, n, :], start=True, stop=False)
                            P.mm(po, qfT[:, tsl], Rb16[0][n][:, :], start=False, stop=False)
                            P.mm(po, qbT[:, tsl], Rb16[1][n][:, :], start=False, stop=True)
                        for i2 in range(2):
                            i = pr * 2 + i2
                            P.act(ob[:, i, :], pO[:, i2 * 256:(i2 + 1) * 256], AF.Identity, accum=stt_[:, i:i + 1])
                            P.act(junk256[:, :], ob[:, i, :], AF.Square, accum=stt_[:, 4 + i:5 + i])
                    if pend is not None:
                        gn_tr(h, *pend)
                    o_b = nxt("og", ogb)
                    gn_norm(h, ob, stt_, o_b, grp)
                    pend = (o_b, grp)
                gn_tr(h, *pend)
                P.dma("sp", ogt_d.v(ogt_d.h.ap()[h * 256:(h + 1) * 256, :].rearrange("(a p) t -> p a t", p=128)),
                      ogT[:, :, :], "ogs")

            def gn_norm(h, ob, stt_, o_b, grp):
                P.ts("dve", stt_[:, 8:12], stt_[:, 0:4], 1.0 / 256, ALU.mult)
                P.tt("dve", stt_[:, 12:16], stt_[:, 8:12], stt_[:, 8:12], ALU.mult)
                P.stt("dve", stt_[:, 12:16], stt_[:, 4:8], 1.0 / 256, stt_[:, 12:16], ALU.mult, ALU.subtract)
                P.ts("dve", stt_[:, 12:16], stt_[:, 12:16], GN_EPS, ALU.add)
                P.tt("pool", stt_[:, 12:16], stt_[:, 12:16], V(cneg.h[:, 0:1].broadcast_to([128, 4]), cneg.reg), ALU.pow)
                P.stt("dve", stt_[:, 16:20], stt_[:, 8:12], -1.0, stt_[:, 12:16], ALU.mult, ALU.mult)
                for i, n in enumerate(grp):
                    P.act(og1[:, i, :], ob[:, i, :], AF.Identity, bias=stt_[:, 16 + i:17 + i], scale=stt_[:, 12 + i:13 + i])
                    P.tt("dve", o_b[:, i, :], og1[:, i, :], sg[:, n, :], ALU.mult)

            def gn_tr(h, o_b, grp):
                pt = nxt("T", psT)
                ptv = V(pt.h[:, :].rearrange("p (c a f) -> p c a f", c=4, a=2), pt.reg)
                for i, n in enumerate(grp):
                    for a in range(2):
                        P.tr(ptv[:, i, a, :], o_b[:, i, a * 128:(a + 1) * 128], ident[:, :])
                for i, n in enumerate(grp):
                    for a in range(2):
                        P.ts("dve", ogT[:, a, n * 128:(n + 1) * 128], ptv[:, i, a, :],
                             gpp[:, h * 2 + a:h * 2 + a + 1], ALU.mult)

            load_w(0)
            for h in range(nheads):
                if h + 1 < nheads:
                    load_w(h + 1)
                precast(6)
                head_consts(h)
                proj(h)
                phaseA(h)
                attn(h)
                if dbg and h == 0:
                    d = dbg_out("qkT", [128, 2 * NTOK], BF16)
                    P.dma("sp", d.v(d.h.ap().rearrange("p (a t) -> p a t", a=2)), qkT[:, :, :], "dbg")
                    d = dbg_out("S", [128, 512], F32)
                    P.dma("sp", d.v(d.h.ap().rearrange("p (a t) -> p a t", a=2)), Sst[:, :, :], "dbg")
                    d = dbg_out("AS", [128, 512], F32)
                    P.dma("sp", d.v(d.h.ap().rearrange("p (a t) -> p a t", a=2)), AS[:, :, :], "dbg")
        precast(100)
        P.barrier()
        if dbg:
            d = dbg_out("ogt", [2048, NTOK], BF16)
            P.dma("sp", d[:, :], ogt_d[:, :], "dbg")
        if stop_after <= 1:
            P.emit()
            return nc, P, dbg_d

        def wsrc2(t, ap2, r0, nk, c0, ncol):
            return V(ap2[r0:r0 + nk * 128, c0:c0 + ncol].rearrange("(k p) n -> p k n", p=128), t.reg)

        wro2 = wro_d.h.ap()[0]
        wco2 = wco_d.h.ap()[0]
        wout2 = wout_d.h.ap()[0]
        w12 = w1_d.h.ap()[0]
        w22 = w2_d.h.ap()[0]

        with ExitStack() as s2:
            psF = [P.ps([128, 512], F32, f"ps2F{i}", s2) for i in range(8)]
            cnt = {"F": 0, "wg": 0, "dg": 0, "sq": 0, "t": 0, "yc": 0}

            def nxt(key, lst):
                i = cnt[key] % len(lst)
                cnt[key] += 1
                return lst[i]

            uT = P.sb([128, 8, 2080], BF16, "uT", s2)
            wg = [P.sb([128, 8, 2, 128], BF16, f"wg{i}", s2) for i in range(3)]
            wco = P.sb([128, 8, D], BF16, "wco", s2)
            cwr = P.sb([31, D], F32, "cwr", s2)
            cw = P.sb([128, 8, 31], F32, "cw", s2)
            dg = [P.sb([128, 31, 128], BF16, f"dg{i}", s2) for i in range(2)]
            cpre = [P.sb([128, 8, 512], F32, f"cpre{i}", s2) for i in range(2)]
            sq = [P.sb([128, 512], F32, f"sq{i}", s2) for i in range(2)]
            onesf = P.sb([128, 128], F32, "onesf", s2)
            hmask = P.sb([128, 32], F32, "hmask", s2)
            hbb = P.sb([128, 8], F32, "hbb", s2)
            tha = [P.sb([128, 512], F32, f"tha{i}", s2) for i in range(2)]
            tga = [P.sb([128, 512], F32, f"tga{i}", s2) for i in range(2)]
            uh = P.sb([128, 32], F32, "uh", s2)
            mean_t = P.sb([128, 512], F32, "mean_t", s2)
            rstd_t = P.sb([128, 512], F32, "rstd_t", s2)
            msq = P.sb([128, 512], F32, "msq", s2)
            tn1 = [P.sb([128, 512], F32, f"tn1{i}", s2) for i in range(2)]
            tn2 = [P.sb([128, 512], F32, f"tn2{i}", s2) for i in range(2)]
            thz = [P.sb([128, 512], F32, f"thz{i}", s2) for i in range(2)]
            cT = P.sb([128, 8, 512], BF16, "cT", s2)
            ycT = [P.sb([128, 8, 512], BF16, f"ycT{i}", s2) for i in range(2)]

            P.dma("pool", wco[:, :, :], wsrc2(wco_d, wco2, 0, 8, 0, D), "wco")
            P.dma("sp", cwr[:, :], cw_d.v(cw_d.h.ap()[0]), "cwr")
            P.dma("sp", hmask[:, :], hmask_d[:, :], "hm")
            P.memset("dve", onesf[:, :], 1.0)
            P.ts("dve", hbb[:, :], pv[:, PV_BGLU + 8:PV_BGLU + 16], 0.5, ALU.mult)
            pcw = nxt("F", psF)
            for cc in range(8):
                P.tr(pcw[:, cc * 32:cc * 32 + 31], cwr[:, cc * 128:(cc + 1) * 128], identf[0:31, 0:31])
            P.ts("dve", cw[:, :, :], V(pcw.h[:, 0:256].rearrange("p (c j) -> p c j", c=8)[:, :, 0:31], pcw.reg),
                 0.5, ALU.mult)

            def load_wg(cc):
                w = wg[cc % 3]
                P.dma("pool", w[:, :, 0, :], wsrc(win_d, 0, 8, 6144 + cc * 128, 128), f"wg{cc % 3}")
                P.dma("pool", w[:, :, 1, :], wsrc(win_d, 0, 8, 7168 + cc * 128, 128), f"wg{cc % 3}")

            load_wg(0)
            load_wg(1)
            for cc in range(8):
                if cc + 2 < 8:
                    load_wg(cc + 2)
                w = wg[cc % 3]
                ba = pv[:, PV_BGLU + cc:PV_BGLU + cc + 1]
                for tt in range(5):
                    if tt < 4:
                        tsl = slice(tt * 512, (tt + 1) * 512)
                        n = 512
                    else:
                        tsl = slice(2048, 2080)
                        n = 32
                    pa = nxt("F", psF)
                    for k in range(8):
                        P.mm(pa[:, 0:n], w[:, k, 0, :], hT[:, k, tsl], start=(k == 0), stop=(k == 7))
                    pb = nxt("F", psF)
                    for k in range(8):
                        P.mm(pb[:, 0:n], w[:, k, 1, :], hT[:, k, tsl], start=(k == 0), stop=(k == 7))
                    t_ = nxt("t", tha)
                    g_ = tga[(cnt["t"] - 1) % 2]
                    P.act(t_[:, 0:n], pb[:, 0:n], AF.Tanh, bias=hbb[:, cc:cc + 1], scale=0.5)
                    P.act(g_[:, 0:n], pa[:, 0:n], AF.Identity, bias=ba, scale=1.0)
                    if tt < 4:
                        P.stt("dve", uT[:, cc, 15 + tt * 512:15 + (tt + 1) * 512], t_[:, :], 1.0, g_[:, :],
                              ALU.add, ALU.mult)
                    else:
                        P.stt("dve", uh[:, :], t_[:, 0:32], 1.0, g_[:, 0:32], ALU.add, ALU.mult)
                        P.tt("dve", uT[:, cc, 0:15], uh[:, 0:15], hmask[:, 0:15], ALU.mult)
                        P.tt("dve", uT[:, cc, 2063:2078], uh[:, 15:30], hmask[:, 15:30], ALU.mult)

            def conv_tt(tt, cp):
                for cc in range(8):
                    d_ = nxt("dg", dg)
                    P.tt("dve", d_[:, :, :], bc(identf[:, :], 1, [128, 31, 128]), bc(cw[:, cc, :], 2, [128, 31, 128]),
                         ALU.mult)
                    pc = nxt("F", psF)
                    for j in range(31):
                        P.mm(pc[:, :], d_[:, j, :], uT[:, cc, tt * 512 + j:tt * 512 + j + 512],
                             start=(j == 0), stop=(j == 30))
                    P.act(cp[:, cc, :], pc[:, :], AF.Identity, bias=pv[:, PV_CONVB + cc:PV_CONVB + cc + 1], scale=1.0)

            def ln_tt(tt, cp):
                pm = nxt("F", psF)
                for cc in range(8):
                    P.mm(pm[:, :], onesf[:, :], cp[:, cc, :], start=(cc == 0), stop=(cc == 7))
                pq = nxt("F", psF)
                for cc in range(8):
                    q_ = nxt("sq", sq)
                    P.act(q_[:, :], cp[:, cc, :], AF.Square)
                    P.mm(pq[:, :], onesf[:, :], q_[:, :], start=(cc == 0), stop=(cc == 7))
                P.act(mean_t[:, :], pm[:, :], AF.Identity, scale=1.0 / D)
                P.act(rstd_t[:, :], pq[:, :], AF.Identity, scale=1.0 / D)
                P.tt("dve", msq[:, :], mean_t[:, :], mean_t[:, :], ALU.mult)
                P.tt("dve", rstd_t[:, :], rstd_t[:, :], msq[:, :], ALU.subtract)
                P.ts("dve", rstd_t[:, :], rstd_t[:, :], GN_EPS, ALU.add)
                P.tt("pool", rstd_t[:, :], rstd_t[:, :], V(cneg.h[:, 0:1].broadcast_to([128, 512]), cneg.reg), ALU.pow)
                for cc in range(8):
                    a1 = nxt("t", tn1)
                    a2 = tn2[(cnt["t"] - 1) % 2]
                    a3 = thz[(cnt["t"] - 1) % 2]
                    P.tt("dve", a1[:, :], cp[:, cc, :], mean_t[:, :], ALU.subtract)
                    P.tt("dve", a2[:, :], a1[:, :], rstd_t[:, :], ALU.mult)
                    P.ts("dve", a1[:, :], a2[:, :], pv[:, PV_LNW + cc:PV_LNW + cc + 1], ALU.mult,
                         pv[:, PV_LNB + cc:PV_LNB + cc + 1], ALU.add)
                    P.act(a3[:, :], a1[:, :], AF.Tanh, scale=0.5)
                    P.stt("dve", cT[:, cc, :], a3[:, :], 1.0, a1[:, :], ALU.add, ALU.mult)

            def yconv_tt(tt):
                yc = nxt("yc", ycT)
                for oc in range(8):
                    py = nxt("F", psF)
                    for cc in range(8):
                        P.mm(py[:, :], wco[:, cc, oc * 128:(oc + 1) * 128], cT[:, cc, :], start=(cc == 0), stop=(cc == 7))
                    P.act(yc[:, oc, :], py[:, :], AF.Identity, bias=pv[:, PV_BCO + oc:PV_BCO + oc + 1], scale=0.5)
                P.dma("sp", yc_d.v(yc_d.h.ap()[:, tt * 512:(tt + 1) * 512].rearrange("(o p) t -> p o t", p=128)),
                      yc[:, :, :], f"ycs{(cnt['yc'] - 1) % 2}")

            conv_tt(0, cpre[0])
            for tt in range(4):
                if tt + 1 < 4:
                    conv_tt(tt + 1, cpre[(tt + 1) % 2])
                ln_tt(tt, cpre[tt % 2])
                yconv_tt(tt)
        P.barrier()
        if dbg:
            d = dbg_out("yc", [D, NTOK], BF16)
            P.dma("sp", d[:, :], yc_d[:, :], "dbg")
        if stop_after <= 2:
            P.emit()
            return nc, P, dbg_d

        with ExitStack() as s3:
            psF = [P.ps([128, 512], F32, f"ps3F{i}", s3) for i in range(7)]
            psT = P.ps([128, 8, 128], BF16, "ps3T", s3)
            cnt = {"F": 0, "ws": 0, "th": 0, "rt": 0}

            def nxt(key, lst):
                i = cnt[key] % len(lst)
                cnt[key] += 1
                return lst[i]

            NWS = 3
            ws = [P.sb([128, 4096], BF16, f"ws{i}", s3) for i in range(NWS)]
            ogt_t = P.sb([128, 16, 512], BF16, "ogt_t", s3)
            yc_t = P.sb([128, 8, 512], BF16, "yc_t", s3)
            thr = [P.sb([128, 512], F32, f"thr{i}", s3) for i in range(2)]
            thc = [P.sb([128, 512], F32, f"thc{i}", s3) for i in range(2)]
            t1m = [P.sb([128, 512], F32, f"t1m{i}", s3) for i in range(2)]
            t2m = [P.sb([128, 512], F32, f"t2m{i}", s3) for i in range(2)]
            mT = P.sb([128, 8, 512], BF16, "mT", s3)
            x_t = P.sb([128, 4, D], F32, "x_t", s3)
            x1_t = P.sb([128, 4, D], F32, "x1_t", s3)
            junk = P.sb([128, D], F32, "junk3", s3)
            ss2 = P.sb([128, 8], F32, "ss2", s3)
            rs2 = P.sb([128, 8], F32, "rs2", s3)
            h2 = P.sb([128, 4, D], BF16, "h2", s3)
            h2T = P.sb([128, 8, 512], BF16, "h2T", s3)
            aT = [P.sb([128, 8, 512], BF16, f"aT{i}", s3) for i in range(4)]
            rtmp = [P.sb([128, 512], F32, f"rtmp{i}", s3) for i in range(2)]
            nfw = P.sb([128, D], F32, "nfw", s3)
            P.dma("sp", nfw[:, :], nfw_d.v(nfw_d.h.ap().rearrange("(a d) -> a d", a=1).partition_broadcast(128)), "nfw")

            uses = []
            for tt in range(4):
                for oc in range(8):
                    uses.append(("g", oc))
                for half in range(2):
                    uses.append(("o", half))
                for hb in range(8):
                    uses.append(("1", hb))
                for half in range(2):
                    for hcb in range(4):
                        uses.append(("2", half, hcb))
            issued = [0]

            def issue(i):
                u = uses[i]
                s_ = ws[i % NWS]
                key = f"ws{i % NWS}"
                if u[0] == "g":
                    src = wg_s[u[1]]
                elif u[0] == "o":
                    src = wo_s[u[1]]
                elif u[0] == "1":
                    src = w1_s[u[1]]
                else:
                    src = w2_s[u[1] * 4 + u[2]]
                P.dma("sp", s_[:, :], src[:, :], key)

            ucnt = [0]

            def wget():
                i = ucnt[0]
                ucnt[0] += 1
                while issued[0] < len(uses) and issued[0] <= i + NWS - 1:
                    issue(issued[0])
                    issued[0] += 1
                return ws[i % NWS]

            for tt in range(4):
                tsl = slice(tt * 512, (tt + 1) * 512)
                P.dma("sp", x_t[:, :, :], x_d.v(x_d.h.ap()[tsl, :].rearrange("(s p) d -> p s d", p=128)), "xl")
                P.dma("sp", ogt_t[:, :, :], ogt_d.v(ogt_d.h.ap()[:, tsl].rearrange("(k p) t -> p k t", p=128)), "ogl")
                P.dma("sp", yc_t[:, :, :], yc_d.v(yc_d.h.ap()[:, tsl].rearrange("(k p) t -> p k t", p=128)), "ycl")
                for oc in range(8):
                    s_ = wget()
                    sgr = V(s_.h[:, 0:1024].rearrange("p (k n) -> p k n", k=8), s_.reg)
                    sgc = V(s_.h[:, 1024:2048].rearrange("p (k n) -> p k n", k=8), s_.reg)
                    swr = V(s_.h[:, 2048:4096].rearrange("p (k n) -> p k n", k=16), s_.reg)
                    pgr = nxt("F", psF)
                    for k in range(8):
                        P.mm(pgr[:, :], sgr[:, k, :], hT[:, k, tsl], start=(k == 0), stop=(k == 7))
                    pgc = nxt("F", psF)
                    for k in range(8):
                        P.mm(pgc[:, :], sgc[:, k, :], hT[:, k, tsl], start=(k == 0), stop=(k == 7))
                    pyr = nxt("F", psF)
                    for k in range(16):
                        P.mm(pyr[:, :], swr[:, k, :], ogt_t[:, k, :], start=(k == 0), stop=(k == 15))
                    i_ = cnt["th"] % 2
                    cnt["th"] += 1
                    P.act(thr[i_][:, :], pgr[:, :], AF.Tanh, scale=0.5)
                    P.act(thc[i_][:, :], pgc[:, :], AF.Tanh, scale=0.5)
                    P.stt("dve", t1m[i_][:, :], thr[i_][:, :], 1.0, pyr[:, :], ALU.add, ALU.mult)
                    P.stt("dve", t2m[i_][:, :], thc[i_][:, :], 1.0, yc_t[:, oc, :], ALU.add, ALU.mult)
                    P.tt("dve", mT[:, oc, :], t1m[i_][:, :], t2m[i_][:, :], ALU.add)
                P.memset("dve", ss2[:, :], 0.0)
                for half in range(2):
                    s_ = wget()
                    wv = V(s_.h[:, :].rearrange("p (k n) -> p k n", k=8), s_.reg)
                    hs = slice(half * 512, (half + 1) * 512)
                    for ts_ in range(4):
                        px = nxt("F", psF)
                        for k in range(8):
                            P.mm(px[:, :], mT[:, k, ts_ * 128:(ts_ + 1) * 128], wv[:, k, :], start=(k == 0), stop=(k == 7))
                        P.stt("dve", x1_t[:, ts_, hs], px[:, :], 0.5, x_t[:, ts_, hs], ALU.mult, ALU.add)
                for ts_ in range(4):
                    P.act(junk[:, :], x1_t[:, ts_, :], AF.Square, accum=ss2[:, ts_:ts_ + 1])
                P.ts("dve", rs2[:, 0:4], ss2[:, 0:4], 1.0 / D, ALU.mult, EPS, ALU.add)
                P.tt("pool", rs2[:, 0:4], rs2[:, 0:4], V(cneg.h[:, 0:1].broadcast_to([128, 4]), cneg.reg), ALU.pow)
                for ts_ in range(4):
                    P.ts("dve", h2[:, ts_, :], x1_t[:, ts_, :], rs2[:, ts_:ts_ + 1], ALU.mult)
                    for k in range(8):
                        P.tr(psT[:, k, :], h2[:, ts_, k * 128:(k + 1) * 128], ident[:, :])
                    P.tt("dve", h2T[:, :, ts_ * 128:(ts_ + 1) * 128], psT[:, :, :],
                         bc(pv[:, PV_N2W:PV_N2W + 8], 2, [128, 8, 128]), ALU.mult)
                for hb in range(8):
                    s_ = wget()
                    wv = V(s_.h[:, :].rearrange("p (k n) -> p k n", k=8), s_.reg)
                    for hci in range(4):
                        hc = hb * 4 + hci
                        pa = nxt("F", psF)
                        for k in range(8):
                            P.mm(pa[:, :], wv[:, k, hci * 128:(hci + 1) * 128], h2T[:, k, :], start=(k == 0), stop=(k == 7))
                        r_ = nxt("rt", rtmp)
                        P.act(r_[:, :], pa[:, :], AF.Relu)
                        P.tt("pool", aT[hc // 8][:, hc % 8, :], r_[:, :], r_[:, :], ALU.mult)
                for half in range(2):
                    hs = slice(half * 512, (half + 1) * 512)
                    pacc = [nxt("F", psF) for _ in range(4)]
                    for hcb in range(4):
                        s_ = wget()
                        wv = V(s_.h[:, :].rearrange("p (k n) -> p k n", k=8), s_.reg)
                        for ts_ in range(4):
                            for i in range(8):
                                P.mm(pacc[ts_][:, :], aT[hcb][:, i, ts_ * 128:(ts_ + 1) * 128], wv[:, i, :],
                                     start=(hcb == 0 and i == 0), stop=(hcb == 3 and i == 7))
                    for ts_ in range(4):
                        P.tt("dve", x1_t[:, ts_, hs], pacc[ts_][:, :], x1_t[:, ts_, hs], ALU.add)
                P.memset("dve", ss2[:, 4:8], 0.0)
                for ts_ in range(4):
                    P.act(junk[:, :], x1_t[:, ts_, :], AF.Square, accum=ss2[:, 4 + ts_:5 + ts_])
                P.ts("dve", rs2[:, 4:8], ss2[:, 4:8], 1.0 / D, ALU.mult, EPS, ALU.add)
                P.tt("pool", rs2[:, 4:8], rs2[:, 4:8], V(cneg.h[:, 0:1].broadcast_to([128, 4]), cneg.reg), ALU.pow)
                for ts_ in range(4):
                    P.stt("dve", x_t[:, ts_, :], x1_t[:, ts_, :], rs2[:, 4 + ts_:5 + ts_], nfw[:, :], ALU.mult, ALU.mult)
                P.dma("sp", out_d.v(out_d.h.ap()[tsl, :].rearrange("(s p) d -> p s d", p=128)), x_t[:, :, :], "outs")
        P.emit()
        return nc, P, dbg_d


def host_consts():
    p = np.arange(128, dtype=np.float32)[:, None]
    i = np.arange(128, dtype=np.float32)[None, :]
    cst = np.zeros((128, CST_N), np.float32)
    cst[:, C_P1:C_P1 + 128] = np.maximum(i - p, 0)
    cst[:, C_P2:C_P2 + 128] = np.maximum(p - i, 0)
    cst[:, C_C1:C_C1 + 128] = i + 1
    cst[:, C_C2:C_C2 + 128] = 128 - i
    cst[:, C_P] = p[:, 0]
    cst[:, C_127P] = 127 - p[:, 0]
    n = np.arange(NCH, dtype=np.float32)[None, :]
    cst[:, C_TF:C_TF + 16] = 2047 - (n * 128 + p)
    cst[:, C_TB:C_TB + 16] = n * 128 + p
    return cst


def core_inputs(inputs, c):
    b, j = divmod(c, 4)
    s0 = j * NTOK
    x = np.asarray(inputs["x"])
    m = {}
    m["x"] = np.ascontiguousarray(x[b, s0:s0 + NTOK])
    xh = np.zeros((32, D), np.float32)
    hm = np.zeros((128, 32), np.float32)
    if j > 0:
        xh[0:15] = x[b, s0 - 15:s0]
        hm[:, 0:15] = 1.0
    if j < 3:
        xh[15:30] = x[b, s0 + NTOK:s0 + NTOK + 15]
        hm[:, 15:30] = 1.0
    m["xh"] = xh
    m["hmask"] = hm
    inv_freq = (np.float32(10000.0) ** (-np.arange(0, 128, 2, dtype=np.float32) / np.float32(128))).astype(np.float32)
    pos = (s0 + np.arange(NTOK, dtype=np.float32))
    ang = (pos[:, None] * inv_freq[None, :]).astype(np.float32)
    cos = np.cos(ang.astype(np.float64)).astype(np.float32).reshape(NCH, 128, 64).transpose(1, 0, 2)
    sin = np.sin(ang.astype(np.float64)).astype(np.float32).reshape(NCH, 128, 64).transpose(1, 0, 2)
    m["rope"] = np.ascontiguousarray(np.stack([cos, sin], 1).reshape(128, 2 * NCH * 64))
    m["cst"] = host_consts()
    cce = np.zeros((128, 8), np.float32)
    ccm = np.zeros((128, 8), np.float32)
    for r in range(4):
        if r < j:
            cce[:, r] = NTOK * (j - 1 - r)
            ccm[:, r] = 1.0
        if r > j:
            cce[:, 4 + r] = NTOK * (r - j - 1)
            ccm[:, 4 + r] = 1.0
    m["cce"] = cce
    m["ccm"] = ccm
    for k in ("norm1_w", "w_in", "ret_decay_raw", "ret_gn_w", "w_ret_o", "b_glu", "conv_w", "conv_b",
              "conv_ln_w", "conv_ln_b", "w_conv_o", "b_conv_o", "w_out", "norm2_w", "w_mlp1", "w_mlp2",
              "norm_f_w"):
        m[k] = np.ascontiguousarray(np.asarray(inputs[k], dtype=np.float32))
    return m


def kernel(**inputs):
    nc, P, _ = build()
    in_maps = [core_inputs(inputs, c) for c in range(NCORES)]
    res = run_bass_kernel_spmd(nc, in_maps, core_ids=list(range(NCORES)))
    out = np.zeros((2, 8192, D), np.float32)
    for c in range(NCORES):
        b, j = divmod(c, 4)
        out[b, j * NTOK:(j + 1) * NTOK] = res.results[c]["out"]
    return out
```

```python
import math
from contextlib import ExitStack

import numpy as np
import concourse.bass as bass
import concourse.mybir as mybir
from concourse.bass_utils import run_bass_kernel_spmd

F32 = mybir.dt.float32
BF16 = mybir.dt.bfloat16
AF = mybir.ActivationFunctionType
ALU = mybir.AluOpType
AX = mybir.AxisListType

NCORES = 8
D = 1024
NTOK = 2048
NCH = 16
NH = 8
KS = 128 ** -0.5
LNKS = math.log(KS)
EPS = 1e-6
GN_EPS = 1e-5


class Reg:
    __slots__ = ("name", "w", "r", "dw", "dr")

    def __init__(self, name):
        self.name = name
        self.w = {}
        self.r = {}
        self.dw = {}
        self.dr = {}


class V:
    __slots__ = ("ap", "reg")

    def __init__(self, ap, reg):
        self.ap = ap
        self.reg = reg

    def __getitem__(self, idx):
        return V(self.ap[idx], self.reg)


class T:
    def __init__(self, h, name):
        self.h = h
        self.reg = Reg(name)

    def __getitem__(self, idx):
        return V(self.h[idx], self.reg)

    def v(self, ap):
        return V(ap, self.reg)

    def carve(self, idx, name):
        return V(self.h[idx], Reg(name))


class Prog:
    ENGS = ("pe", "act", "dve", "pool", "sp")

    def __init__(self, nc):
        self.nc = nc
        self.ops = {e: [] for e in self.ENGS}
        self.dma_cnt = {}
        self.n_t = 0

    def sb(self, shape, dt, name, stack=None):
        self.n_t += 1
        name = f"{name}_{self.n_t}"
        if stack is None:
            h = self.nc.alloc_sbuf_tensor(name, list(shape), dt)
        else:
            h = stack.enter_context(self.nc.sbuf_tensor(name, list(shape), dt))
        return T(h, name)

    def ps(self, shape, dt, name, stack):
        self.n_t += 1
        name = f"{name}_{self.n_t}"
        h = stack.enter_context(self.nc.psum_tensor(name, list(shape), dt))
        return T(h, name)

    def dram(self, shape, dt, name, kind="Internal"):
        return T(self.nc.dram_tensor(name, list(shape), dt, kind=kind), name)

    def op(self, eng, emit, reads=(), writes=(), dma_key=None, inc=16, part=False):
        lst = self.ops[eng]
        idx = len(lst)
        is_dma = dma_key is not None
        de, dd = {}, {}
        for r in reads:
            r = r.reg
            for e, i in r.w.items():
                if e != eng or eng != "pe" or is_dma:
                    if de.get(e, -1) < i:
                        de[e] = i
            for k, c in r.dw.items():
                if dd.get(k, -1) < c:
                    dd[k] = c
        for w in writes:
            w = w.reg
            for e, i in list(w.r.items()) + list(w.w.items()):
                if e != eng or is_dma or eng != "pe":
                    if de.get(e, -1) < i:
                        de[e] = i
            for k, c in list(w.dr.items()) + list(w.dw.items()):
                if part and k == dma_key and k in w.dw:
                    continue
                if dd.get(k, -1) < c:
                    dd[k] = c
        cnt = None
        if is_dma:
            cnt = self.dma_cnt.get(dma_key, 0) + inc
            self.dma_cnt[dma_key] = cnt
        for r in reads:
            r = r.reg
            if is_dma:
                r.dr[dma_key] = cnt
            else:
                r.r[eng] = idx
        for w in writes:
            w = w.reg
            w.r.clear()
            w.dr.clear()
            w.w.clear()
            w.dw.clear()
            if is_dma:
                w.dw[dma_key] = cnt
            else:
                w.w[eng] = idx
        lst.append(dict(emit=emit, de=de, dd=dd, dma_key=dma_key, inc=inc))
        return idx

    def barrier(self):
        last = {}
        for e in self.ENGS:
            nd = [i for i, o in enumerate(self.ops[e]) if o["dma_key"] is None and o["emit"] is not None]
            if nd:
                last[e] = nd[-1]
        for e in self.ENGS:
            de = {e2: i for e2, i in last.items() if e2 != e}
            self.ops[e].append(dict(emit=None, de=de, dd=dict(self.dma_cnt), dma_key=None, inc=0))

    def emit(self):
        nc = self.nc
        sig = {e: set() for e in self.ENGS}
        for e in self.ENGS:
            for o in self.ops[e]:
                for e2, i2 in o["de"].items():
                    sig[e2].add(i2)
        last = {}
        for e in self.ENGS:
            if e != "sp":
                nd = [i for i, o in enumerate(self.ops[e]) if o["dma_key"] is None and o["emit"] is not None]
                if nd:
                    sig[e].add(nd[-1])
                    last[e] = nd[-1]
        cnts = {}
        for e in self.ENGS:
            c = 0
            m = {}
            for i in range(len(self.ops[e])):
                if i in sig[e]:
                    c += 1
                    m[i] = c
            cnts[e] = m
        esem = {e: nc.alloc_semaphore(f"sem_{e}") for e in self.ENGS}
        dsem = {k: nc.alloc_semaphore(f"dsem_{k}") for k in self.dma_cnt}
        self.stats = {e: len(self.ops[e]) for e in self.ENGS}
        self.stats["nsem"] = len(esem) + len(dsem)
        prog = self

        def replay(e, eng):
            seen_e = {}
            seen_d = {}
            nwait = 0
            for i, o in enumerate(prog.ops[e]):
                for e2, i2 in o["de"].items():
                    v = cnts[e2][i2]
                    if seen_e.get(e2, 0) < v:
                        eng.wait_ge(esem[e2], v)
                        seen_e[e2] = v
                        nwait += 1
                for k, c in o["dd"].items():
                    if seen_d.get(k, 0) < c:
                        eng.wait_ge(dsem[k], c)
                        seen_d[k] = c
                        nwait += 1
                if o["emit"] is None:
                    continue
                ins = o["emit"](eng)
                if o["dma_key"] is not None:
                    ins.then_inc(dsem[o["dma_key"]], o["inc"])
                    assert i not in sig[e]
                elif i in sig[e]:
                    ins.then_inc(esem[e], 1)
            if e == "sp":
                for e2, i2 in last.items():
                    eng.wait_ge(esem[e2], cnts[e2][i2])
                for k, c in prog.dma_cnt.items():
                    eng.wait_ge(dsem[k], c)
            prog.stats["w_" + e] = nwait

        with nc.Block() as block:
            @block.tensor
            def _(eng):
                replay("pe", eng)

            @block.scalar
            def _(eng):
                replay("act", eng)

            @block.vector
            def _(eng):
                replay("dve", eng)

            @block.gpsimd
            def _(eng):
                replay("pool", eng)

            @block.sync
            def _(eng):
                replay("sp", eng)

    def dma(self, q, out, in_, key, part=False, **kw):
        return self.op(q, lambda e: e.dma_start(out=out.ap, in_=in_.ap, **kw),
                       reads=[in_], writes=[out], dma_key=key, part=part)

    def mm(self, out, lhsT, rhs, start=True, stop=True):
        return self.op("pe", lambda e: e.matmul(out.ap, lhsT.ap, rhs.ap, start=start, stop=stop),
                       reads=[lhsT, rhs], writes=[out])

    def tr(self, out, in_, ident):
        return self.op("pe", lambda e: e.transpose(out.ap, in_.ap, ident.ap),
                       reads=[in_, ident], writes=[out])

    def act(self, out, in_, func, bias=None, scale=None, accum=None):
        kw = {}
        rd = [in_]
        wr = [out]
        if bias is not None:
            if isinstance(bias, V):
                kw["bias"] = bias.ap
                rd.append(bias)
            else:
                kw["bias"] = bias
        if scale is not None:
            if isinstance(scale, V):
                kw["scale"] = scale.ap
                rd.append(scale)
            else:
                kw["scale"] = scale
        if accum is not None:
            kw["accum_out"] = accum.ap
            wr.append(accum)
        return self.op("act", lambda e: e.activation(out.ap, in_.ap, func, **kw), reads=rd, writes=wr)

    def tt(self, eng, out, a, b, op):
        return self.op(eng, lambda e: e.tensor_tensor(out.ap, a.ap, b.ap, op), reads=[a, b], writes=[out])

    def ts(self, eng, out, a, s1, op0, s2=None, op1=None):
        rd = [a]
        s1a = s1.ap if isinstance(s1, V) else s1
        s2a = s2.ap if isinstance(s2, V) else s2
        if isinstance(s1, V):
            rd.append(s1)
        if isinstance(s2, V):
            rd.append(s2)
        kw = {}
        if op1 is not None:
            kw["op1"] = op1
        return self.op(eng, lambda e: e.tensor_scalar(out.ap, a.ap, s1a, s2a, op0, **kw), reads=rd, writes=[out])

    def stt(self, eng, out, a, s, b, op0, op1):
        rd = [a, b]
        sa = s.ap if isinstance(s, V) else s
        if isinstance(s, V):
            rd.append(s)
        return self.op(eng, lambda e: e.scalar_tensor_tensor(out.ap, a.ap, sa, b.ap, op0, op1),
                       reads=rd, writes=[out])

    def red(self, eng, out, in_, op=None):
        op = op or ALU.add
        return self.op(eng, lambda e: e.tensor_reduce(out.ap, in_.ap, AX.X, op), reads=[in_], writes=[out])

    def cp(self, eng, out, in_):
        if eng == "act":
            return self.op(eng, lambda e: e.copy(out.ap, in_.ap), reads=[in_], writes=[out])
        return self.op(eng, lambda e: e.tensor_copy(out.ap, in_.ap), reads=[in_], writes=[out])

    def memset(self, eng, out, val):
        return self.op(eng, lambda e: e.memset(out.ap, val), writes=[out])


def bc(v, axis, shape):
    return V(v.ap.unsqueeze(axis).broadcast_to(list(shape)), v.reg)


C_P1, C_P2, C_C1, C_C2 = 0, 128, 256, 384
C_P, C_127P, C_TF, C_TB = 512, 513, 514, 530
CST_N = 546
PV_N1W, PV_BGLU, PV_CONVB, PV_LNW, PV_LNB, PV_BCO, PV_N2W = 0, 8, 24, 32, 40, 48, 56
PV_GNW = 64
PV_ROWS = 80


def build(stop_after=99, dbg=False, nheads=NH, sub=9):
    nc = bass.Bass("TRN2", target_bir_lowering=False)
    P = Prog(nc)
    ein = "ExternalInput"
    x_d = P.dram([NTOK, D], F32, "x", ein)
    xh_d = P.dram([32, D], F32, "xh", ein)
    hmask_d = P.dram([128, 32], F32, "hmask", ein)
    rope_d = P.dram([128, 2 * NCH * 64], F32, "rope", ein)
    cst_d = P.dram([128, CST_N], F32, "cst", ein)
    cce_d = P.dram([128, 8], F32, "cce", ein)
    ccm_d = P.dram([128, 8], F32, "ccm", ein)
    n1w_d = P.dram([1, D], F32, "norm1_w", ein)
    win_d = P.dram([1, D, 10240], F32, "w_in", ein)
    raw_d = P.dram([1, 2, NH], F32, "ret_decay_raw", ein)
    gnw_d = P.dram([1, 2048], F32, "ret_gn_w", ein)
    wro_d = P.dram([1, 2048, D], F32, "w_ret_o", ein)
    bglu_d = P.dram([1, 2048], F32, "b_glu", ein)
    cw_d = P.dram([1, 31, D], F32, "conv_w", ein)
    cb_d = P.dram([1, D], F32, "conv_b", ein)
    lnw_d = P.dram([1, D], F32, "conv_ln_w", ein)
    lnb_d = P.dram([1, D], F32, "conv_ln_b", ein)
    wco_d = P.dram([1, D, D], F32, "w_conv_o", ein)
    bco_d = P.dram([1, D], F32, "b_conv_o", ein)
    wout_d = P.dram([1, D, D], F32, "w_out", ein)
    n2w_d = P.dram([1, D], F32, "norm2_w", ein)
    w1_d = P.dram([1, D, 4096], F32, "w_mlp1", ein)
    w2_d = P.dram([1, 4096, D], F32, "w_mlp2", ein)
    nfw_d = P.dram([D], F32, "norm_f_w", ein)
    out_d = P.dram([NTOK, D], F32, "out", "ExternalOutput")
    ogt_d = P.dram([2048, NTOK], BF16, "ogt_scr")
    yc_d = P.dram([D, NTOK], BF16, "yc_scr")
    ain_d = [P.dram([256, 256], F32, f"ain{i}") for i in range(2)]
    aout_d = [P.dram([4 * 256, 256], F32, f"aout{i}") for i in range(2)]
    dbg_d = {}

    def dbg_out(name, shape, dt=F32):
        dbg_d[name] = P.dram(shape, dt, "dbg_" + name, "ExternalOutput")
        return dbg_d[name]

    win = win_d.h.ap()[0]

    def wsrc(t, r0, nk, c0, ncol, ap2=None):
        a = ap2 if ap2 is not None else win
        return V(a[r0:r0 + nk * 128, c0:c0 + ncol].rearrange("(k p) n -> p k n", p=128), t.reg)

    wg_s = [P.dram([128, 4096], BF16, f"wg_s{i}") for i in range(8)]
    wo_s = [P.dram([128, 4096], BF16, f"wo_s{i}") for i in range(2)]
    w1_s = [P.dram([128, 4096], BF16, f"w1_s{i}") for i in range(8)]
    w2_s = [P.dram([128, 4096], BF16, f"w2_s{i}") for i in range(8)]
    pc_jobs = []

    def pcj(dst_t, c0, nk, src_ap2, r0, col0, ncol):
        dst = V(dst_t.h.ap()[:, c0:c0 + nk * ncol].rearrange("p (k n) -> p k n", k=nk), dst_t.reg)
        src = V(src_ap2[r0:r0 + nk * 128, col0:col0 + ncol].rearrange("(k p) n -> p k n", p=128), Reg("wsrc"))
        pc_jobs.append((dst, src))

    for oc in range(8):
        pcj(wg_s[oc], 0, 8, win, 0, 8192 + oc * 128, 128)
        pcj(wg_s[oc], 1024, 8, win, 0, 9216 + oc * 128, 128)
        pcj(wg_s[oc], 2048, 16, wro_d.h.ap()[0], 0, oc * 128, 128)
    for half in range(2):
        pcj(wo_s[half], 0, 8, wout_d.h.ap()[0], 0, half * 512, 512)
    for hb in range(8):
        pcj(w1_s[hb], 0, 8, w1_d.h.ap()[0], 0, hb * 512, 512)
    for half in range(2):
        for hcb in range(4):
            pcj(w2_s[half * 4 + hcb], 0, 8, w2_d.h.ap()[0], hcb * 1024, half * 512, 512)

    def precast(n):
        for _ in range(n):
            if pc_jobs:
                dst, src = pc_jobs.pop(0)
                P.dma("pool", dst, src, "pc", part=True)

    top = ExitStack()
    with top:
        hT = P.sb([128, 8, 2080], BF16, "hT", top)
        ident = P.sb([128, 128], BF16, "ident", top)
        identf = P.sb([128, 128], F32, "identf", top)
        pv = P.sb([128, PV_ROWS], F32, "pv", top)
        cneg = P.sb([128, 1], F32, "cneg", top)

        P.memset("pool", identf[:, :], 0.0)
        P.op("pool", lambda e: e.affine_select(identf.h[:, :], identf.h[:, :], pattern=[[-1, 128]],
                                               compare_op=ALU.not_equal, fill=1.0, base=0,
                                               channel_multiplier=1),
             reads=[identf], writes=[identf])
        P.cp("dve", ident[:, :], identf[:, :])
        P.memset("pool", cneg[:, :], -0.5)

        with ExitStack() as s0:
            pvr = P.sb([PV_ROWS, 128], F32, "pvr", s0)
            for (row, dd_, n) in ((PV_N1W, n1w_d, 8), (PV_BGLU, bglu_d, 16), (PV_CONVB, cb_d, 8),
                                  (PV_LNW, lnw_d, 8), (PV_LNB, lnb_d, 8), (PV_BCO, bco_d, 8),
                                  (PV_N2W, n2w_d, 8), (PV_GNW, gnw_d, 16)):
                P.dma("sp", pvr[row:row + n, :],
                      dd_.v(dd_.h.ap().rearrange("a (k p) -> (a k) p", p=128)), "par")
            ptp = P.ps([128, 512], F32, "ptp", s0)
            P.tr(ptp[:, 0:PV_ROWS], pvr[:, :], identf[0:PV_ROWS, 0:PV_ROWS])
            P.cp("dve", pv[:, :], ptp[:, 0:PV_ROWS])

            xt = [P.sb([128, 4, D], F32, f"xt{i}", s0) for i in range(2)]
            xsb = [P.sb([128, 4, D], BF16, f"xsb{i}", s0) for i in range(2)]
            junk = P.sb([128, D], F32, "junk", s0)
            ss = P.sb([128, 20], F32, "ss", s0)
            rstd = P.sb([128, 20], F32, "rstd", s0)
            ptr = [P.ps([128, 8, 128], BF16, f"ptr{i}", s0) for i in range(2)]
            P.memset("dve", ss[:, :], 0.0)
            P.memset("pool", xt[0][:, :, :], 0.0)
            for g in range(5):
                b = g % 2
                if g < 4:
                    P.dma("sp", xt[b][:, :, :],
                          x_d.v(x_d.h.ap()[g * 512:(g + 1) * 512, :].rearrange("(c p) d -> p c d", p=128)),
                          f"x{b}")
                    nsub = 4
                else:
                    P.dma("sp", xt[b][0:32, 0, :], xh_d[:, :], f"x{b}")
                    nsub = 1
                for c in range(nsub):
                    P.act(junk[:, :], xt[b][:, c, :], AF.Square, accum=ss[:, g * 4 + c:g * 4 + c + 1])
                sl = slice(g * 4, g * 4 + nsub)
                P.ts("dve", rstd[:, sl], ss[:, sl], 1.0 / D, ALU.mult, EPS, ALU.add)
                P.tt("pool", rstd[:, sl], rstd[:, sl], V(cneg.h[:, 0:1].broadcast_to([128, nsub]), cneg.reg), ALU.pow)
                for c in range(nsub):
                    P.ts("dve", xsb[b][:, c, :], xt[b][:, c, :], rstd[:, g * 4 + c:g * 4 + c + 1], ALU.mult)
                    pt = ptr[c % 2]
                    for k in range(8):
                        P.tr(pt[:, k, :], xsb[b][:, c, k * 128:(k + 1) * 128], ident[:, :])
                    n1 = bc(pv[:, PV_N1W:PV_N1W + 8], 2, [128, 8, 128])
                    if g < 4:
                        col = (g * 4 + c) * 128
                        P.tt("dve", hT[:, :, col:col + 128], pt[:, :, :], n1, ALU.mult)
                    else:
                        n1h = bc(pv[:, PV_N1W:PV_N1W + 8], 2, [128, 8, 32])
                        P.tt("dve", hT[:, :, 2048:2080], pt[:, :, 0:32], n1h, ALU.mult)
        P.barrier()
        if dbg:
            d = dbg_out("hT", [128, 8 * 2080], BF16)
            P.dma("sp", d.v(d.h.ap().rearrange("p (k t) -> p k t", k=8)), hT[:, :, :], "dbg")
        if stop_after <= 0:
            P.emit()
            return nc, P, dbg_d

        with ExitStack() as s1:
            cst = P.sb([128, CST_N], F32, "cst", s1)
            P.dma("sp", cst[:, :], cst_d[:, :], "par")
            rope = P.sb([128, 2, NCH, 64], F32, "rope", s1)
            P.dma("sp", rope[:, :, :, :], rope_d.v(rope_d.h.ap().rearrange("p (a n f) -> p a n f", a=2, n=NCH)), "par")
            gpp = P.sb([128, 16], F32, "gpp", s1)
            P.ts("dve", gpp[:, :], pv[:, PV_GNW:PV_GNW + 16], 0.5, ALU.mult)
            lg = P.sb([128, 16], F32, "lg", s1)
            P.dma("sp", lg[:, :], raw_d.v(raw_d.h.ap().rearrange("a d h -> a (d h)").partition_broadcast(128)), "par")
            cce = P.sb([128, 8], F32, "cce", s1)
            ccm = P.sb([128, 8], F32, "ccm", s1)
            P.dma("sp", cce[:, :], cce_d[:, :], "par")
            P.dma("sp", ccm[:, :], ccm_d[:, :], "par")
            for t_ in (cst, rope, lg, cce, ccm):
                t_.reg.dw["par"] = P.dma_cnt["par"]

            P.act(lg[:, :], lg[:, :], AF.Exp)
            P.ts("dve", lg[:, :], lg[:, :], -1.0, ALU.mult)
            DT = P.sb([128, 128], F32, "DT", s1)
            XF = P.sb([128, 128], F32, "XF", s1)
            XB = P.sb([128, 128], F32, "XB", s1)
            zf = P.sb([128, NH], F32, "zf", s1)
            zb = P.sb([128, NH], F32, "zb", s1)
            zF = P.sb([128, NH, NCH], F32, "zF", s1)
            zB = P.sb([128, NH, NCH], F32, "zB", s1)
            g128 = P.sb([128, 16], F32, "g128", s1)
            cc = P.sb([128, 2, 4, NH], F32, "cc", s1)
            tmpc = P.sb([128, 128], F32, "tmpc", s1)
            for h in range(NH):
                lf = lg[:, h:h + 1]
                lb = lg[:, 8 + h:9 + h]
                P.ts("dve", tmpc[:, 0:16], cst[:, C_TF:C_TF + 16], lf, ALU.mult, LNKS, ALU.add)
                P.act(zF[:, h, :], tmpc[:, 0:16], AF.Exp)
                P.ts("dve", tmpc[:, 16:32], cst[:, C_TB:C_TB + 16], lb, ALU.mult, LNKS, ALU.add)
                P.act(zB[:, h, :], tmpc[:, 16:32], AF.Exp)
                P.ts("dve", tmpc[:, 32:33], cst[:, C_127P:C_127P + 1], lf, ALU.mult, LNKS, ALU.add)
                P.act(zf[:, h:h + 1], tmpc[:, 32:33], AF.Exp)
                P.ts("dve", tmpc[:, 33:34], cst[:, C_P:C_P + 1], lb, ALU.mult, LNKS, ALU.add)
                P.act(zb[:, h:h + 1], tmpc[:, 33:34], AF.Exp)
            P.act(g128[:, :], lg[:, :], AF.Exp, scale=128.0)
            for d_ in range(2):
                for r in range(4):
                    P.ts("dve", tmpc[:, 64:72], lg[:, d_ * 8:d_ * 8 + 8], cce[:, d_ * 4 + r:d_ * 4 + r + 1], ALU.mult)
                    P.act(tmpc[:, 72:80], tmpc[:, 64:72], AF.Exp)
                    P.ts("dve", cc[:, d_, r, :], tmpc[:, 72:80], ccm[:, d_ * 4 + r:d_ * 4 + r + 1], ALU.mult)

            psF = [P.ps([128, 512], F32, f"psF{i}", s1) for i in range(6)]
            psT = [P.ps([128, 1024], BF16, f"psT{i}", s1) for i in range(2)]

            wb = [P.sb([128, 8, 768], BF16, f"wb{i}", s1) for i in range(2)]
            qkT = P.sb([128, 2, NTOK], BF16, "qkT", s1)
            qfT = P.sb([128, NTOK], BF16, "qfT", s1)
            qbT = P.sb([128, NTOK], BF16, "qbT", s1)
            kfb = P.sb([128, 2, NCH, 128], BF16, "kfb", s1)
            vb = P.sb([128, NCH, 256], BF16, "vb", s1)
            sg = P.sb([128, NCH, 256], BF16, "sg", s1)
            kFB = P.sb([128, 2, NCH, 128], BF16, "kFB", s1)
            rawqk = [P.sb([128, 4, 2, 2, 64], F32, f"rawqk{i}", s1) for i in range(2)]
            th = [P.sb([128, 256], F32, f"th{i}", s1) for i in range(2)]
            ra = P.sb([128, 4, 2, 64], F32, "ra", s1)
            rb = P.sb([128, 4, 2, 64], F32, "rb", s1)
            rc = P.sb([128, 4, 2, 64], F32, "rc", s1)
            rd_ = P.sb([128, 4, 2, 64], F32, "rd", s1)
            rot = [P.sb([128, 4, 2, 2, 64], BF16, f"rot{i}", s1) for i in range(2)]
            AS = P.sb([128, 2, 256], F32, "AS", s1)
            agin = P.sb([128, 4, 2, 256], F32, "agin", s1)
            Sst = P.sb([128, 2, 256], F32, "Sst", s1)
            Rf32 = [P.sb([128, 2, 256], F32, f"Rf32_{i}", s1) for i in range(2)]
            Rb16 = [[P.sb([128, 256], BF16, f"Rb16_{d_}_{n}", s1) for n in range(NCH)] for d_ in range(2)]
            sd = [P.sb([128, 4, 128], BF16, f"sd{i}", s1) for i in range(4)]
            osb = [P.sb([128, 4, 256], F32, f"osb{i}", s1) for i in range(2)]
            stats = [P.sb([128, 20], F32, f"stats{i}", s1) for i in range(2)]
            junk256 = P.sb([128, 256], F32, "junk256", s1)
            og1 = P.sb([128, 4, 256], F32, "og1", s1)
            ogb = [P.sb([128, 4, 256], BF16, f"ogb{i}", s1) for i in range(2)]
            ogT = P.sb([128, 2, NTOK], BF16, "ogT", s1)

            def head_consts(h):
                lf = lg[:, h:h + 1]
                lb = lg[:, 8 + h:9 + h]
                P.ts("dve", tmpc[:, :], cst[:, C_P1:C_P1 + 128], lf, ALU.mult, LNKS, ALU.add)
                P.stt("dve", tmpc[:, :], cst[:, C_P2:C_P2 + 128], lb, tmpc[:, :], ALU.mult, ALU.add)
                P.act(DT[:, :], tmpc[:, :], AF.Exp)
                P.act(XF[:, :], cst[:, C_C1:C_C1 + 128], AF.Exp, scale=lf)
                P.act(XB[:, :], cst[:, C_C2:C_C2 + 128], AF.Exp, scale=lb)

            cnt = {"F": 0, "T": 0, "sd": 0, "rq": 0, "ro": 0, "ob": 0, "og": 0}

            def nxt(key, lst):
                i = cnt[key] % len(lst)
                cnt[key] += 1
                return lst[i]

            def load_w(h):
                w = wb[h % 2]
                k_ = f"w{h % 2}"
                P.dma("pool", w[:, :, 0:128], wsrc(win_d, 0, 8, h * 128, 128), k_)
                P.dma("pool", w[:, :, 128:256], wsrc(win_d, 0, 8, 1024 + h * 128, 128), k_)
                P.dma("pool", w[:, :, 256:512], wsrc(win_d, 0, 8, 2048 + h * 256, 256), k_)
                P.dma("pool", w[:, :, 512:768], wsrc(win_d, 0, 8, 4096 + h * 256, 256), k_)

            def proj(h):
                w = wb[h % 2]
                ros = {}
                for cg in range(4):
                    rq = nxt("rq", rawqk)
                    ro = nxt("ro", rot)
                    ros[cg] = ro
                    for ci in range(4):
                        c = cg * 4 + ci
                        pa = nxt("F", psF)
                        for k in range(8):
                            P.mm(pa[:, :], hT[:, k, c * 128:(c + 1) * 128], w[:, k, 0:512],
                                 start=(k == 0), stop=(k == 7))
                        pb = nxt("F", psF)
                        for k in range(8):
                            P.mm(pb[:, 0:256], hT[:, k, c * 128:(c + 1) * 128], w[:, k, 512:768],
                                 start=(k == 0), stop=(k == 7))
                        P.cp("act", V(rq.h[:, ci].rearrange("p a b f -> p (a b f)"), rq.reg), pa[:, 0:256])
                        P.cp("act", vb[:, c, :], pa[:, 256:512])
                        t_ = th[c % 2]
                        P.act(t_[:, :], pb[:, 0:256], AF.Tanh, scale=0.5)
                        P.stt("dve", sg[:, c, :], t_[:, :], 1.0, pb[:, 0:256], ALU.add, ALU.mult)
                    if cg > 0:
                        proj_tr(h, cg - 1, ros[cg - 1])
                    cos4 = bc(rope[:, 0, cg * 4:(cg + 1) * 4, :], 2, [128, 4, 2, 64])
                    sin4 = bc(rope[:, 1, cg * 4:(cg + 1) * 4, :], 2, [128, 4, 2, 64])
                    t1 = rq[:, :, :, 0, :]
                    t2 = rq[:, :, :, 1, :]
                    P.tt("dve", ra[:, :, :, :], t1, cos4, ALU.mult)
                    P.tt("dve", rb[:, :, :, :], t2, sin4, ALU.mult)
                    P.tt("dve", ro[:, :, :, 0, :], ra[:, :, :, :], rb[:, :, :, :], ALU.subtract)
                    P.tt("pool", rc[:, :, :, :], t2, cos4, ALU.mult)
                    P.tt("pool", rd_[:, :, :, :], t1, sin4, ALU.mult)
                    P.tt("pool", ro[:, :, :, 1, :], rc[:, :, :, :], rd_[:, :, :, :], ALU.add)
                    krot = V(ro.h[:, :, 1].rearrange("p c a f -> p c (a f)"), ro.reg)
                    csl = slice(cg * 4, (cg + 1) * 4)
                    P.ts("dve", kfb[:, 0, csl, :], krot, zf[:, h:h + 1], ALU.mult)
                    P.ts("dve", kfb[:, 1, csl, :], krot, zb[:, h:h + 1], ALU.mult)
                    P.tt("pool", kFB[:, 0, csl, :], krot, bc(zF[:, h, csl], 2, [128, 4, 128]), ALU.mult)
                    P.tt("pool", kFB[:, 1, csl, :], krot, bc(zB[:, h, csl], 2, [128, 4, 128]), ALU.mult)
                proj_tr(h, 3, ros[3])

            def proj_tr(h, cg, ro):
                pt = nxt("T", psT)
                ptv = V(pt.h[:, :].rearrange("p (c a f) -> p c a f", c=4, a=2), pt.reg)
                for ci in range(4):
                    for a in range(2):
                        P.tr(ptv[:, ci, a, :], V(ro.h[:, ci, a].rearrange("p a f -> p (a f)"), ro.reg),
                             ident[:, :])
                tsl = slice(cg * 512, (cg + 1) * 512)
                P.cp("act", V(qkT.h[:, :, tsl].rearrange("p a (c f) -> p c a f", c=4), qkT.reg), ptv[:, :, :, :])
                q4 = V(qkT.h[:, 0, tsl].rearrange("p (c f) -> p c f", c=4), qkT.reg)
                P.tt("pool", V(qfT.h[:, tsl].rearrange("p (c f) -> p c f", c=4), qfT.reg), q4,
                     bc(XF[:, :], 1, [128, 4, 128]), ALU.mult)
                P.tt("pool", V(qbT.h[:, tsl].rearrange("p (c f) -> p c f", c=4), qbT.reg), q4,
                     bc(XB[:, :], 1, [128, 4, 128]), ALU.mult)

            def phaseA(h):
                s = h % 2
                pu = nxt("F", psF)
                for d_ in range(2):
                    for c in range(NCH):
                        P.mm(pu[:, d_ * 256:(d_ + 1) * 256], kFB[:, d_, c, :], vb[:, c, :],
                             start=(c == 0), stop=(c == NCH - 1))
                P.cp("act", V(AS.h[:, :, :].rearrange("p d v -> p (d v)"), AS.reg), pu[:, :])
                P.dma("sp", ain_d[s].v(ain_d[s].h.ap().rearrange("(d p) v -> p d v", p=128)), AS[:, :, :], f"ain{s}")
                P.op("pool", lambda e: e.collective_compute(
                    "AllGather", ALU.bypass, replica_groups=[[0, 1, 2, 3], [4, 5, 6, 7]],
                    ins=[ain_d[s].h.ap().opt()], outs=[aout_d[s].h.ap().opt()]),
                    reads=[ain_d[s]], writes=[aout_d[s]], dma_key=f"cc{s}", inc=1)
                P.dma("sp", agin[:, :, :, :],
                      aout_d[s].v(aout_d[s].h.ap().rearrange("(r d p) v -> p r d v", p=128, d=2)), "agl")

            def attn(h):
                order = sorted(range(NCH), key=lambda n: (max(n, NCH - 1 - n), n))
                for gi in range(4):
                    grp = order[gi * 4:(gi + 1) * 4]
                    pS = nxt("F", psF)
                    for i, n in enumerate(grp):
                        tsl = slice(n * 128, (n + 1) * 128)
                        P.mm(pS[:, i * 128:(i + 1) * 128], qkT[:, 1, tsl], qkT[:, 0, tsl])
                    P.tt("dve", sd[gi][:, :, :], V(pS.h[:, :].rearrange("p (c f) -> p c f", c=4), pS.reg),
                         bc(DT[:, :], 1, [128, 4, 128]), ALU.mult)
                for d_ in range(2):
                    P.ts("dve", Sst[:, d_, :], agin[:, 0, d_, :], cc[:, d_, 0, h:h + 1], ALU.mult)
                    for r in range(1, 4):
                        P.stt("dve", Sst[:, d_, :], agin[:, r, d_, :], cc[:, d_, r, h:h + 1], Sst[:, d_, :],
                              ALU.mult, ALU.add)
                cur = Sst
                P.cp("act", Rb16[0][0][:, :], Sst[:, 0, :])
                P.cp("act", Rb16[1][NCH - 1][:, :], Sst[:, 1, :])
                for step in range(NCH - 1):
                    pu = nxt("F", psF)
                    nw = Rf32[step % 2]
                    for d_ in range(2):
                        n = step if d_ == 0 else NCH - 1 - step
                        P.mm(pu[:, d_ * 256:(d_ + 1) * 256], kfb[:, d_, n, :], vb[:, n, :])
                    for d_ in range(2):
                        n = step if d_ == 0 else NCH - 1 - step
                        nn = n + 1 if d_ == 0 else n - 1
                        P.stt("dve", nw[:, d_, :], cur[:, d_, :], g128[:, d_ * 8 + h:d_ * 8 + h + 1],
                              pu[:, d_ * 256:(d_ + 1) * 256], ALU.mult, ALU.add)
                        P.cp("act", Rb16[d_][nn][:, :], nw[:, d_, :])
                    cur = nw
                order = sorted(range(NCH), key=lambda n: (max(n, NCH - 1 - n), n))
                pend = None
                for gi in range(4):
                    grp = order[gi * 4:(gi + 1) * 4]
                    sd4 = sd[gi]
                    ob = nxt("ob", osb)
                    stt_ = stats[gi % 2]
                    P.memset("dve", stt_[:, 0:8], 0.0)
                    for pr in range(2):
                        pO = nxt("F", psF)
                        for i2 in range(2):
                            i = pr * 2 + i2
                            n = grp[i]
                            tsl = slice(n * 128, (n + 1) * 128)
                            po = pO[:, i2 * 256:(i2 + 1) * 256]
                            P.mm(po, sd4[:, i, :], vb[:, n, :], start=True, stop=False)
                            P.mm(po, qfT[:, tsl], Rb16[0][n][:, :], start=False, stop=False)
                            P.mm(po, qbT[:, tsl], Rb16[1][n][:, :], start=False, stop=True)
                        for i2 in range(2):
                            i = pr * 2 + i2
                            P.act(ob[:, i, :], pO[:, i2 * 256:(i2 + 1) * 256], AF.Identity, accum=stt_[:, i:i + 1])
                            P.act(junk256[:, :], ob[:, i, :], AF.Square, accum=stt_[:, 4 + i:5 + i])
                    if pend is not None:
                        gn_tr(h, *pend)
                    o_b = nxt("og", ogb)
                    gn_norm(h, ob, stt_, o_b, grp)
                    pend = (o_b, grp)
                gn_tr(h, *pend)
                P.dma("sp", ogt_d.v(ogt_d.h.ap()[h * 256:(h + 1) * 256, :].rearrange("(a p) t -> p a t", p=128)),
                      ogT[:, :, :], "ogs")

            def gn_norm(h, ob, stt_, o_b, grp):
                P.ts("dve", stt_[:, 8:12], stt_[:, 0:4], 1.0 / 256, ALU.mult)
                P.tt("dve", stt_[:, 12:16], stt_[:, 8:12], stt_[:, 8:12], ALU.mult)
                P.stt("dve", stt_[:, 12:16], stt_[:, 4:8], 1.0 / 256, stt_[:, 12:16], ALU.mult, ALU.subtract)
                P.ts("dve", stt_[:, 12:16], stt_[:, 12:16], GN_EPS, ALU.add)
                P.tt("pool", stt_[:, 12:16], stt_[:, 12:16], V(cneg.h[:, 0:1].broadcast_to([128, 4]), cneg.reg), ALU.pow)
                P.stt("dve", stt_[:, 16:20], stt_[:, 8:12], -1.0, stt_[:, 12:16], ALU.mult, ALU.mult)
                for i, n in enumerate(grp):
                    P.act(og1[:, i, :], ob[:, i, :], AF.Identity, bias=stt_[:, 16 + i:17 + i], scale=stt_[:, 12 + i:13 + i])
                    P.tt("dve", o_b[:, i, :], og1[:, i, :], sg[:, n, :], ALU.mult)

            def gn_tr(h, o_b, grp):
                pt = nxt("T", psT)
                ptv = V(pt.h[:, :].rearrange("p (c a f) -> p c a f", c=4, a=2), pt.reg)
                for i, n in enumerate(grp):
                    for a in range(2):
                        P.tr(ptv[:, i, a, :], o_b[:, i, a * 128:(a + 1) * 128], ident[:, :])
                for i, n in enumerate(grp):
                    P.ts("dve", ogT[:, 0, n * 128:(n + 1) * 128], ptv[:, i, 0, :],
                         gpp[:, h * 2:h * 2 + 1], ALU.mult)
                    P.act(ogT[:, 1, n * 128:(n + 1) * 128], ptv[:, i, 1, :], AF.Identity,
                          scale=gpp[:, h * 2 + 1:h * 2 + 2])

            load_w(0)
            for h in range(nheads):
                if h + 1 < nheads:
                    load_w(h + 1)
                precast(6)
                head_consts(h)
                proj(h)
                phaseA(h)
                attn(h)
                if dbg and h == 0:
                    d = dbg_out("qkT", [128, 2 * NTOK], BF16)
                    P.dma("sp", d.v(d.h.ap().rearrange("p (a t) -> p a t", a=2)), qkT[:, :, :], "dbg")
                    d = dbg_out("S", [128, 512], F32)
                    P.dma("sp", d.v(d.h.ap().rearrange("p (a t) -> p a t", a=2)), Sst[:, :, :], "dbg")
                    d = dbg_out("AS", [128, 512], F32)
                    P.dma("sp", d.v(d.h.ap().rearrange("p (a t) -> p a t", a=2)), AS[:, :, :], "dbg")
        precast(100)
        P.barrier()
        if dbg:
            d = dbg_out("ogt", [2048, NTOK], BF16)
            P.dma("sp", d[:, :], ogt_d[:, :], "dbg")
        if stop_after <= 1:
            P.emit()
            return nc, P, dbg_d

        def wsrc2(t, ap2, r0, nk, c0, ncol):
            return V(ap2[r0:r0 + nk * 128, c0:c0 + ncol].rearrange("(k p) n -> p k n", p=128), t.reg)

        wro2 = wro_d.h.ap()[0]
        wco2 = wco_d.h.ap()[0]
        wout2 = wout_d.h.ap()[0]
        w12 = w1_d.h.ap()[0]
        w22 = w2_d.h.ap()[0]

        with ExitStack() as s2:
            psF = [P.ps([128, 512], F32, f"ps2F{i}", s2) for i in range(8)]
            cnt = {"F": 0, "wg": 0, "dg": 0, "sq": 0, "t": 0, "yc": 0}

            def nxt(key, lst):
                i = cnt[key] % len(lst)
                cnt[key] += 1
                return lst[i]

            uT = P.sb([128, 8, 2080], BF16, "uT", s2)
            wg = [P.sb([128, 8, 2, 128], BF16, f"wg{i}", s2) for i in range(3)]
            wco = P.sb([128, 8, D], BF16, "wco", s2)
            cwr = P.sb([31, D], F32, "cwr", s2)
            cw = P.sb([128, 8, 31], F32, "cw", s2)
            dg = [P.sb([128, 31, 128], BF16, f"dg{i}", s2) for i in range(2)]
            cpre = [P.sb([128, 8, 512], F32, f"cpre{i}", s2) for i in range(2)]
            sq = [P.sb([128, 512], F32, f"sq{i}", s2) for i in range(2)]
            onesf = P.sb([128, 128], F32, "onesf", s2)
            hmask = P.sb([128, 32], F32, "hmask", s2)
            hbb = P.sb([128, 8], F32, "hbb", s2)
            tha = [P.sb([128, 512], F32, f"tha{i}", s2) for i in range(2)]
            tga = [P.sb([128, 512], F32, f"tga{i}", s2) for i in range(2)]
            uh = P.sb([128, 32], F32, "uh", s2)
            mean_t = P.sb([128, 512], F32, "mean_t", s2)
            rstd_t = P.sb([128, 512], F32, "rstd_t", s2)
            msq = P.sb([128, 512], F32, "msq", s2)
            tn1 = [P.sb([128, 512], F32, f"tn1{i}", s2) for i in range(2)]
            tn2 = [P.sb([128, 512], F32, f"tn2{i}", s2) for i in range(2)]
            thz = [P.sb([128, 512], F32, f"thz{i}", s2) for i in range(2)]
            cT = P.sb([128, 8, 512], BF16, "cT", s2)
            ycT = [P.sb([128, 8, 512], BF16, f"ycT{i}", s2) for i in range(2)]

            P.dma("pool", wco[:, :, :], wsrc2(wco_d, wco2, 0, 8, 0, D), "wco")
            P.dma("sp", cwr[:, :], cw_d.v(cw_d.h.ap()[0]), "cwr")
            P.dma("sp", hmask[:, :], hmask_d[:, :], "hm")
            P.memset("dve", onesf[:, :], 1.0)
            ceps = P.sb([128, 1], F32, "ceps", s2)
            P.memset("dve", ceps[:, :], GN_EPS)
            P.ts("dve", hbb[:, :], pv[:, PV_BGLU + 8:PV_BGLU + 16], 0.5, ALU.mult)
            pcw = nxt("F", psF)
            for cc in range(8):
                P.tr(pcw[:, cc * 32:cc * 32 + 31], cwr[:, cc * 128:(cc + 1) * 128], identf[0:31, 0:31])
            P.ts("dve", cw[:, :, :], V(pcw.h[:, 0:256].rearrange("p (c j) -> p c j", c=8)[:, :, 0:31], pcw.reg),
                 0.5, ALU.mult)

            def load_wg(cc):
                w = wg[cc % 3]
                P.dma("pool", w[:, :, 0, :], wsrc(win_d, 0, 8, 6144 + cc * 128, 128), f"wg{cc % 3}")
                P.dma("pool", w[:, :, 1, :], wsrc(win_d, 0, 8, 7168 + cc * 128, 128), f"wg{cc % 3}")

            load_wg(0)
            load_wg(1)
            for cc in range(8):
                if cc + 2 < 8:
                    load_wg(cc + 2)
                w = wg[cc % 3]
                ba = pv[:, PV_BGLU + cc:PV_BGLU + cc + 1]
                for tt in range(5):
                    if tt < 4:
                        tsl = slice(tt * 512, (tt + 1) * 512)
                        n = 512
                    else:
                        tsl = slice(2048, 2080)
                        n = 32
                    pa = nxt("F", psF)
                    for k in range(8):
                        P.mm(pa[:, 0:n], w[:, k, 0, :], hT[:, k, tsl], start=(k == 0), stop=(k == 7))
                    pb = nxt("F", psF)
                    for k in range(8):
                        P.mm(pb[:, 0:n], w[:, k, 1, :], hT[:, k, tsl], start=(k == 0), stop=(k == 7))
                    t_ = nxt("t", tha)
                    g_ = tga[(cnt["t"] - 1) % 2]
                    P.act(t_[:, 0:n], pb[:, 0:n], AF.Tanh, bias=hbb[:, cc:cc + 1], scale=0.5)
                    P.act(g_[:, 0:n], pa[:, 0:n], AF.Identity, bias=ba, scale=1.0)
                    if tt < 4:
                        P.stt("dve", uT[:, cc, 15 + tt * 512:15 + (tt + 1) * 512], t_[:, :], 1.0, g_[:, :],
                              ALU.add, ALU.mult)
                    else:
                        P.stt("dve", uh[:, :], t_[:, 0:32], 1.0, g_[:, 0:32], ALU.add, ALU.mult)
                        P.tt("dve", uT[:, cc, 0:15], uh[:, 0:15], hmask[:, 0:15], ALU.mult)
                        P.tt("dve", uT[:, cc, 2063:2078], uh[:, 15:30], hmask[:, 15:30], ALU.mult)

            def conv_tt(tt, cp):
                for cc in range(8):
                    d_ = nxt("dg", dg)
                    P.tt("dve", d_[:, :, :], bc(identf[:, :], 1, [128, 31, 128]), bc(cw[:, cc, :], 2, [128, 31, 128]),
                         ALU.mult)
                    pc = nxt("F", psF)
                    for j in range(31):
                        P.mm(pc[:, :], d_[:, j, :], uT[:, cc, tt * 512 + j:tt * 512 + j + 512],
                             start=(j == 0), stop=(j == 30))
                    P.act(cp[:, cc, :], pc[:, :], AF.Identity, bias=pv[:, PV_CONVB + cc:PV_CONVB + cc + 1], scale=1.0)

            def ln_tt(tt, cp):
                pm = nxt("F", psF)
                for cc in range(8):
                    P.mm(pm[:, :], onesf[:, :], cp[:, cc, :], start=(cc == 0), stop=(cc == 7))
                pq = nxt("F", psF)
                for cc in range(8):
                    q_ = nxt("sq", sq)
                    P.act(q_[:, :], cp[:, cc, :], AF.Square)
                    P.mm(pq[:, :], onesf[:, :], q_[:, :], start=(cc == 0), stop=(cc == 7))
                P.act(mean_t[:, :], pm[:, :], AF.Identity, scale=1.0 / D)
                P.act(rstd_t[:, :], pq[:, :], AF.Identity, scale=1.0 / D)
                P.tt("dve", msq[:, :], mean_t[:, :], mean_t[:, :], ALU.mult)
                P.tt("dve", rstd_t[:, :], rstd_t[:, :], msq[:, :], ALU.subtract)
                P.act(msq[:, :], rstd_t[:, :], AF.Sqrt, bias=ceps[:, 0:1], scale=1.0)
                P.op("dve", lambda e: e.reciprocal(rstd_t.h[:, :], msq.h[:, :]), reads=[msq], writes=[rstd_t])
                for cc in range(8):
                    a1 = nxt("t", tn1)
                    a2 = tn2[(cnt["t"] - 1) % 2]
                    a3 = thz[(cnt["t"] - 1) % 2]
                    P.tt("dve", a1[:, :], cp[:, cc, :], mean_t[:, :], ALU.subtract)
                    P.tt("dve", a2[:, :], a1[:, :], rstd_t[:, :], ALU.mult)
                    P.ts("dve", a1[:, :], a2[:, :], pv[:, PV_LNW + cc:PV_LNW + cc + 1], ALU.mult,
                         pv[:, PV_LNB + cc:PV_LNB + cc + 1], ALU.add)
                    P.act(a3[:, :], a1[:, :], AF.Tanh, scale=0.5)
                    P.stt("dve", cT[:, cc, :], a3[:, :], 1.0, a1[:, :], ALU.add, ALU.mult)

            def yconv_tt(tt):
                yc = nxt("yc", ycT)
                for oc in range(8):
                    py = nxt("F", psF)
                    for cc in range(8):
                        P.mm(py[:, :], wco[:, cc, oc * 128:(oc + 1) * 128], cT[:, cc, :], start=(cc == 0), stop=(cc == 7))
                    P.act(yc[:, oc, :], py[:, :], AF.Identity, bias=pv[:, PV_BCO + oc:PV_BCO + oc + 1], scale=0.5)
                P.dma("sp", yc_d.v(yc_d.h.ap()[:, tt * 512:(tt + 1) * 512].rearrange("(o p) t -> p o t", p=128)),
                      yc[:, :, :], f"ycs{(cnt['yc'] - 1) % 2}")

            conv_tt(0, cpre[0])
            for tt in range(4):
                if tt + 1 < 4:
                    conv_tt(tt + 1, cpre[(tt + 1) % 2])
                ln_tt(tt, cpre[tt % 2])
                yconv_tt(tt)
        P.barrier()
        if dbg:
            d = dbg_out("yc", [D, NTOK], BF16)
            P.dma("sp", d[:, :], yc_d[:, :], "dbg")
        if stop_after <= 2:
            P.emit()
            return nc, P, dbg_d

        with ExitStack() as s3:
            psF = [P.ps([128, 512], F32, f"ps3F{i}", s3) for i in range(7)]
            psT = P.ps([128, 8, 128], BF16, "ps3T", s3)
            cnt = {"F": 0, "ws": 0, "th": 0, "rt": 0}

            def nxt(key, lst):
                i = cnt[key] % len(lst)
                cnt[key] += 1
                return lst[i]

            NWS = 3
            ws = [P.sb([128, 4096], BF16, f"ws{i}", s3) for i in range(NWS)]
            ogt_t = P.sb([128, 16, 512], BF16, "ogt_t", s3)
            yc_t = P.sb([128, 8, 512], BF16, "yc_t", s3)
            thr = [P.sb([128, 512], F32, f"thr{i}", s3) for i in range(2)]
            thc = [P.sb([128, 512], F32, f"thc{i}", s3) for i in range(2)]
            t1m = [P.sb([128, 512], F32, f"t1m{i}", s3) for i in range(2)]
            t2m = [P.sb([128, 512], F32, f"t2m{i}", s3) for i in range(2)]
            mT = P.sb([128, 8, 512], BF16, "mT", s3)
            x_t = P.sb([128, 4, D], F32, "x_t", s3)
            x1_t = P.sb([128, 4, D], F32, "x1_t", s3)
            junk = P.sb([128, D], F32, "junk3", s3)
            ss2 = P.sb([128, 8], F32, "ss2", s3)
            rs2 = P.sb([128, 8], F32, "rs2", s3)
            h2 = P.sb([128, 4, D], BF16, "h2", s3)
            h2T = P.sb([128, 8, 512], BF16, "h2T", s3)
            aT = [P.sb([128, 8, 512], BF16, f"aT{i}", s3) for i in range(4)]
            rtmp = [P.sb([128, 512], F32, f"rtmp{i}", s3) for i in range(2)]
            nfw = P.sb([128, D], F32, "nfw", s3)
            P.dma("sp", nfw[:, :], nfw_d.v(nfw_d.h.ap().rearrange("(a d) -> a d", a=1).partition_broadcast(128)), "nfw")

            uses = []
            for tt in range(4):
                for oc in range(8):
                    uses.append(("g", oc))
                for half in range(2):
                    uses.append(("o", half))
                for hb in range(8):
                    uses.append(("1", hb))
                for half in range(2):
                    for hcb in range(4):
                        uses.append(("2", half, hcb))
            issued = [0]

            def issue(i):
                u = uses[i]
                s_ = ws[i % NWS]
                key = f"ws{i % NWS}"
                if u[0] == "g":
                    src = wg_s[u[1]]
                elif u[0] == "o":
                    src = wo_s[u[1]]
                elif u[0] == "1":
                    src = w1_s[u[1]]
                else:
                    src = w2_s[u[1] * 4 + u[2]]
                P.dma("sp", s_[:, :], src[:, :], key)

            ucnt = [0]

            def wget():
                i = ucnt[0]
                ucnt[0] += 1
                while issued[0] < len(uses) and issued[0] <= i + NWS - 1:
                    issue(issued[0])
                    issued[0] += 1
                return ws[i % NWS]

            def load_oy(t_):
                sl_ = slice(t_ * 512, (t_ + 1) * 512)
                P.dma("sp", ogt_t[:, :, :], ogt_d.v(ogt_d.h.ap()[:, sl_].rearrange("(k p) t -> p k t", p=128)), "ogl")
                P.dma("sp", yc_t[:, :, :], yc_d.v(yc_d.h.ap()[:, sl_].rearrange("(k p) t -> p k t", p=128)), "ycl")

            for tt in range(4):
                tsl = slice(tt * 512, (tt + 1) * 512)
                P.dma("sp", x_t[:, :, :], x_d.v(x_d.h.ap()[tsl, :].rearrange("(s p) d -> p s d", p=128)), "xl")
                if tt == 0:
                    load_oy(0)
                for oc in range(8):
                    s_ = wget()
                    sgr = V(s_.h[:, 0:1024].rearrange("p (k n) -> p k n", k=8), s_.reg)
                    sgc = V(s_.h[:, 1024:2048].rearrange("p (k n) -> p k n", k=8), s_.reg)
                    swr = V(s_.h[:, 2048:4096].rearrange("p (k n) -> p k n", k=16), s_.reg)
                    pgr = nxt("F", psF)
                    for k in range(8):
                        P.mm(pgr[:, :], sgr[:, k, :], hT[:, k, tsl], start=(k == 0), stop=(k == 7))
                    pgc = nxt("F", psF)
                    for k in range(8):
                        P.mm(pgc[:, :], sgc[:, k, :], hT[:, k, tsl], start=(k == 0), stop=(k == 7))
                    pyr = nxt("F", psF)
                    for k in range(16):
                        P.mm(pyr[:, :], swr[:, k, :], ogt_t[:, k, :], start=(k == 0), stop=(k == 15))
                    i_ = cnt["th"] % 2
                    cnt["th"] += 1
                    P.act(thr[i_][:, :], pgr[:, :], AF.Tanh, scale=0.5)
                    P.act(thc[i_][:, :], pgc[:, :], AF.Tanh, scale=0.5)
                    P.stt("dve", t1m[i_][:, :], thr[i_][:, :], 1.0, pyr[:, :], ALU.add, ALU.mult)
                    P.stt("dve", t2m[i_][:, :], thc[i_][:, :], 1.0, yc_t[:, oc, :], ALU.add, ALU.mult)
                    P.tt("dve", mT[:, oc, :], t1m[i_][:, :], t2m[i_][:, :], ALU.add)
                if tt + 1 < 4:
                    load_oy(tt + 1)
                P.memset("dve", ss2[:, :], 0.0)
                for half in range(2):
                    s_ = wget()
                    wv = V(s_.h[:, :].rearrange("p (k n) -> p k n", k=8), s_.reg)
                    hs = slice(half * 512, (half + 1) * 512)
                    for ts_ in range(4):
                        px = nxt("F", psF)
                        for k in range(8):
                            P.mm(px[:, :], mT[:, k, ts_ * 128:(ts_ + 1) * 128], wv[:, k, :], start=(k == 0), stop=(k == 7))
                        P.stt("dve", x1_t[:, ts_, hs], px[:, :], 0.5, x_t[:, ts_, hs], ALU.mult, ALU.add)
                for ts_ in range(4):
                    P.act(junk[:, :], x1_t[:, ts_, :], AF.Square, accum=ss2[:, ts_:ts_ + 1])
                P.ts("dve", rs2[:, 0:4], ss2[:, 0:4], 1.0 / D, ALU.mult, EPS, ALU.add)
                P.tt("pool", rs2[:, 0:4], rs2[:, 0:4], V(cneg.h[:, 0:1].broadcast_to([128, 4]), cneg.reg), ALU.pow)
                for ts_ in range(4):
                    P.ts("dve", h2[:, ts_, :], x1_t[:, ts_, :], rs2[:, ts_:ts_ + 1], ALU.mult)
                    for k in range(8):
                        P.tr(psT[:, k, :], h2[:, ts_, k * 128:(k + 1) * 128], ident[:, :])
                    P.tt("dve", h2T[:, :, ts_ * 128:(ts_ + 1) * 128], psT[:, :, :],
                         bc(pv[:, PV_N2W:PV_N2W + 8], 2, [128, 8, 128]), ALU.mult)
                for hb in range(8):
                    s_ = wget()
                    wv = V(s_.h[:, :].rearrange("p (k n) -> p k n", k=8), s_.reg)
                    for hci in range(4):
                        hc = hb * 4 + hci
                        pa = nxt("F", psF)
                        for k in range(8):
                            P.mm(pa[:, :], wv[:, k, hci * 128:(hci + 1) * 128], h2T[:, k, :], start=(k == 0), stop=(k == 7))
                        r_ = nxt("rt", rtmp)
                        P.act(r_[:, :], pa[:, :], AF.Relu)
                        P.tt("pool", aT[hc // 8][:, hc % 8, :], r_[:, :], r_[:, :], ALU.mult)
                for half in range(2):
                    hs = slice(half * 512, (half + 1) * 512)
                    pacc = [nxt("F", psF) for _ in range(4)]
                    for hcb in range(4):
                        s_ = wget()
                        wv = V(s_.h[:, :].rearrange("p (k n) -> p k n", k=8), s_.reg)
                        for ts_ in range(4):
                            for i in range(8):
                                P.mm(pacc[ts_][:, :], aT[hcb][:, i, ts_ * 128:(ts_ + 1) * 128], wv[:, i, :],
                                     start=(hcb == 0 and i == 0), stop=(hcb == 3 and i == 7))
                    for ts_ in range(4):
                        P.tt("dve", x1_t[:, ts_, hs], pacc[ts_][:, :], x1_t[:, ts_, hs], ALU.add)
                P.memset("dve", ss2[:, 4:8], 0.0)
                for ts_ in range(4):
                    P.act(junk[:, :], x1_t[:, ts_, :], AF.Square, accum=ss2[:, 4 + ts_:5 + ts_])
                P.ts("dve", rs2[:, 4:8], ss2[:, 4:8], 1.0 / D, ALU.mult, EPS, ALU.add)
                P.tt("pool", rs2[:, 4:8], rs2[:, 4:8], V(cneg.h[:, 0:1].broadcast_to([128, 4]), cneg.reg), ALU.pow)
                for ts_ in range(4):
                    P.stt("dve", x_t[:, ts_, :], x1_t[:, ts_, :], rs2[:, 4 + ts_:5 + ts_], nfw[:, :], ALU.mult, ALU.mult)
                P.dma("sp", out_d.v(out_d.h.ap()[tsl, :].rearrange("(s p) d -> p s d", p=128)), x_t[:, :, :], "outs")
        P.emit()
        return nc, P, dbg_d


def host_consts():
    p = np.arange(128, dtype=np.float32)[:, None]
    i = np.arange(128, dtype=np.float32)[None, :]
    cst = np.zeros((128, CST_N), np.float32)
    cst[:, C_P1:C_P1 + 128] = np.maximum(i - p, 0)
    cst[:, C_P2:C_P2 + 128] = np.maximum(p - i, 0)
    cst[:, C_C1:C_C1 + 128] = i + 1
    cst[:, C_C2:C_C2 + 128] = 128 - i
    cst[:, C_P] = p[:, 0]
    cst[:, C_127P] = 127 - p[:, 0]
    n = np.arange(NCH, dtype=np.float32)[None, :]
    cst[:, C_TF:C_TF + 16] = 2047 - (n * 128 + p)
    cst[:, C_TB:C_TB + 16] = n * 128 + p
    return cst


def core_inputs(inputs, c):
    b, j = divmod(c, 4)
    s0 = j * NTOK
    x = np.asarray(inputs["x"])
    m = {}
    m["x"] = np.ascontiguousarray(x[b, s0:s0 + NTOK])
    xh = np.zeros((32, D), np.float32)
    hm = np.zeros((128, 32), np.float32)
    if j > 0:
        xh[0:15] = x[b, s0 - 15:s0]
        hm[:, 0:15] = 1.0
    if j < 3:
        xh[15:30] = x[b, s0 + NTOK:s0 + NTOK + 15]
        hm[:, 15:30] = 1.0
    m["xh"] = xh
    m["hmask"] = hm
    inv_freq = (np.float32(10000.0) ** (-np.arange(0, 128, 2, dtype=np.float32) / np.float32(128))).astype(np.float32)
    pos = (s0 + np.arange(NTOK, dtype=np.float32))
    ang = (pos[:, None] * inv_freq[None, :]).astype(np.float32)
    cos = np.cos(ang.astype(np.float64)).astype(np.float32).reshape(NCH, 128, 64).transpose(1, 0, 2)
    sin = np.sin(ang.astype(np.float64)).astype(np.float32).reshape(NCH, 128, 64).transpose(1, 0, 2)
    m["rope"] = np.ascontiguousarray(np.stack([cos, sin], 1).reshape(128, 2 * NCH * 64))
    m["cst"] = host_consts()
    cce = np.zeros((128, 8), np.float32)
    ccm = np.zeros((128, 8), np.float32)
    for r in range(4):
        if r < j:
            cce[:, r] = NTOK * (j - 1 - r)
            ccm[:, r] = 1.0
        if r > j:
            cce[:, 4 + r] = NTOK * (r - j - 1)
            ccm[:, 4 + r] = 1.0
    m["cce"] = cce
    m["ccm"] = ccm
    for k in ("norm1_w", "w_in", "ret_decay_raw", "ret_gn_w", "w_ret_o", "b_glu", "conv_w", "conv_b",
              "conv_ln_w", "conv_ln_b", "w_conv_o", "b_conv_o", "w_out", "norm2_w", "w_mlp1", "w_mlp2",
              "norm_f_w"):
        m[k] = np.ascontiguousarray(np.asarray(inputs[k], dtype=np.float32))
    return m


def kernel(**inputs):
    nc, P, _ = build()
    in_maps = [core_inputs(inputs, c) for c in range(NCORES)]
    res = run_bass_kernel_spmd(nc, in_maps, core_ids=list(range(NCORES)))
    out = np.zeros((2, 8192, D), np.float32)
    for c in range(NCORES):
        b, j = divmod(c, 4)
        out[b, j * NTOK:(j + 1) * NTOK] = res.results[c]["out"]
    return out
```
